# Optimizing a Trainium2 kernel written in Bass

```python
import math
import jax
import jax.numpy as jnp
from jax import lax
import numpy as np

D_MODEL = 2048
BATCH = 4
SEQ = 2048
DEPTH = 2

M_HEADS = 4
M_DH = 256
M_W = M_HEADS * M_DH
CONV_K = 4
A_HEADS = 8
A_DH = 64
A_W = A_HEADS * 2 * A_DH
ROT_DIM = A_DH // 4
ROPE_THETA = 500000.0
Q_BLOCK = 128
G_HEADS = 8
G_DK = 128
G_DV = 128
G_WK = G_HEADS * G_DK
G_WV = G_HEADS * G_DV
CHUNK = 64
N_BRANCH = 3
D_FF = 4 * D_MODEL
ALPHA = (2 * DEPTH) ** 0.25
BETA = (8 * DEPTH) ** -0.25
LN_EPS = 1e-5
NORM_EPS = 1e-6
NEG_BIG = -1e30
F_FLOOR = 1e-30

IN_SIZES = (M_W, M_W, M_W, M_W, M_HEADS, M_HEADS,
            A_W, A_W, A_W,
            G_WK, G_WK, G_WV, G_WV,
            N_BRANCH * D_MODEL)
IN_WIDTH = sum(IN_SIZES)

kernel_name = 'hybrid_mlstm_diffattn_hgrn2_deepnorm'


def layer_norm(x, w, b):
    xf = x.astype(jnp.float32)
    mu = jnp.mean(xf, -1, keepdims=True)
    var = jnp.mean(jnp.square(xf - mu), -1, keepdims=True)
    return ((xf - mu) * lax.rsqrt(var + LN_EPS)).astype(x.dtype) * w + b


def head_rms_norm(x, w):
    xf = x.astype(jnp.float32)
    H, d = x.shape[-2:]
    y = xf * lax.rsqrt(jnp.mean(jnp.square(xf), -1, keepdims=True) + NORM_EPS)
    return y * w.astype(jnp.float32).reshape(H, d)


def causal_dwconv(x, w, b):
    C = x.shape[-1]
    y = lax.conv_general_dilated(x, w[:, None, :].astype(x.dtype), window_strides=(1,),
                                 padding=[(CONV_K - 1, 0)],
                                 dimension_numbers=('NWC', 'WIO', 'NWC'),
                                 feature_group_count=C)
    return y + b


def rope_cos_sin(positions):
    inv = jnp.power(jnp.float32(ROPE_THETA), -jnp.arange(0, ROT_DIM, 2, dtype=jnp.float32) / ROT_DIM)
    ang = positions.astype(jnp.float32)[..., None] * inv
    return jnp.cos(ang), jnp.sin(ang)


def partial_rope(x, cos, sin):
    half = ROT_DIM // 2
    c = cos[:, :, None, None, :].astype(x.dtype)
    s = sin[:, :, None, None, :].astype(x.dtype)
    x1, x2, rest = x[..., :half], x[..., half:ROT_DIM], x[..., ROT_DIM:]
    return jnp.concatenate([x1 * c - x2 * s, x2 * c + x1 * s, rest], axis=-1)


def to_chunks(t):
    B, S, H = t.shape[:3]
    t = t.reshape((B, S // CHUNK, CHUNK, H) + t.shape[3:])
    return jnp.moveaxis(t, (1, 3), (0, 2))


def from_chunks(t):
    n, B, H, C = t.shape[:4]
    t = jnp.moveaxis(t, (0, 2), (1, 3))
    return t.reshape((B, n * C, H) + t.shape[4:])


def mlstm_chunkwise(q, k, v, i_pre, f_pre):
    B, S, H, d = q.shape
    f32 = jnp.float32
    q = q.astype(f32)
    k = k.astype(f32) * (d ** -0.5)
    v = v.astype(f32)
    log_f = jax.nn.log_sigmoid(f_pre.astype(f32))
    log_i = i_pre.astype(f32)
    causal = jnp.tril(jnp.ones((CHUNK, CHUNK), dtype=bool))

    def step(carry, inp):
        C_prev, n_prev, m_prev = carry
        qc, kc, vc, ic, lfc = inp
        b = jnp.cumsum(lfc, axis=-1)
        log_d = b[..., :, None] - b[..., None, :] + ic[..., None, :]
        log_d = jnp.where(causal, log_d, NEG_BIG)
        log_inter = b + m_prev[..., None]
        m_t = jnp.maximum(log_inter, jnp.max(log_d, -1))
        w_inter = jnp.exp(log_inter - m_t)
        s = jnp.einsum('bhtd,bhsd->bhts', qc, kc) * jnp.exp(log_d - m_t[..., None])
        num = (jnp.einsum('bhts,bhse->bhte', s, vc)
               + w_inter[..., None] * jnp.einsum('bhtd,bhde->bhte', qc, C_prev))
        den = jnp.sum(s, -1) + w_inter * jnp.einsum('bhtd,bhd->bht', qc, n_prev)
        h = num / jnp.maximum(jnp.abs(den), jnp.exp(-m_t))[..., None]
        g = b[..., -1]
        log_w = g[..., None] - b + ic
        m_new = jnp.maximum(g + m_prev, jnp.max(log_w, -1))
        w_s = jnp.exp(log_w - m_new[..., None])
        decay = jnp.exp(g + m_prev - m_new)
        C_new = decay[..., None, None] * C_prev + jnp.einsum('bhs,bhsd,bhse->bhde', w_s, kc, vc)
        n_new = decay[..., None] * n_prev + jnp.einsum('bhs,bhsd->bhd', w_s, kc)
        return (C_new, n_new, m_new), h

    init = (jnp.zeros((B, H, d, d), f32), jnp.zeros((B, H, d), f32), jnp.zeros((B, H), f32))
    _, h = lax.scan(step, init, (to_chunks(q), to_chunks(k), to_chunks(v),
                                 to_chunks(log_i), to_chunks(log_f)))
    return from_chunks(h)


def hgrn2_chunkwise(q, k, v, log_f):
    B, S, H, dk = q.shape
    dv = v.shape[-1]
    f32 = jnp.float32
    q, k, v, log_f = (t.astype(f32) for t in (q, k, v, log_f))
    causal = jnp.tril(jnp.ones((CHUNK, CHUNK), dtype=bool))[:, :, None]

    def step(S_prev, inp):
        qc, kc, vc, lfc = inp
        b = jnp.cumsum(lfc, axis=2)
        rel = b[:, :, :, None, :] - b[:, :, None, :, :]
        rel = jnp.where(causal, rel, NEG_BIG)
        attn = jnp.einsum('bhtk,bhsk,bhtsk->bhts', qc, kc, jnp.exp(rel))
        o = (jnp.einsum('bhts,bhsv->bhtv', attn, vc)
             + jnp.einsum('bhtk,bhkv->bhtv', qc * jnp.exp(b), S_prev))
        b_last = b[:, :, -1]
        S_new = (jnp.exp(b_last)[..., None] * S_prev
                 + jnp.einsum('bhsk,bhsv->bhkv', kc * jnp.exp(b_last[:, :, None] - b), vc))
        return S_new, o

    _, o = lax.scan(step, jnp.zeros((B, H, dk, dv), f32),
                    (to_chunks(q), to_chunks(k), to_chunks(v), to_chunks(log_f)))
    return from_chunks(o)


def diff_attention(q, k, v, lam):
    B, S, H, _, d = q.shape
    nb = S // Q_BLOCK
    scale = d ** -0.5
    kpos = jnp.arange(S)
    qb = jnp.moveaxis(q.reshape(B, nb, Q_BLOCK, H, 2, d), 1, 0)

    def block(args):
        qblk, start = args
        s = jnp.einsum('bqhnd,bkhnd->bhnqk', qblk, k).astype(jnp.float32) * scale
        qpos = start + jnp.arange(Q_BLOCK)
        s = jnp.where(kpos[None, :] <= qpos[:, None], s, NEG_BIG)
        p = jax.nn.softmax(s, axis=-1)
        a = p[:, :, 0] - lam * p[:, :, 1]
        return jnp.einsum('bhqk,bkhe->bqhe', a.astype(v.dtype), v)

    out = lax.map(block, (qb, jnp.arange(nb) * Q_BLOCK))
    return jnp.moveaxis(out, 0, 1).reshape(B, S, H, 2 * d)


def mixer_sublayer(u, layer_idx, cos, sin, lb, w_in, conv_w, conv_b, gate_b, m_norm_w,
                   lam_vecs, a_norm_w, g_norm_w, p_m, p_a, p_g, w_out):
    B, S, _ = u.shape
    f32 = jnp.float32
    split_at = np.cumsum(IN_SIZES)[:-1].tolist()
    (mq, mk, mv, mo, mi, mf, aq, ak, av, gq, gf, gi, gg, gate_pre) = jnp.split(u @ w_in, split_at, axis=-1)

    qk = jax.nn.silu(causal_dwconv(jnp.concatenate([mq, mk], axis=-1), conv_w, conv_b))
    mq, mk = jnp.split(qk, 2, axis=-1)
    hshape = (B, S, M_HEADS, M_DH)
    hm = mlstm_chunkwise(mq.reshape(hshape), mk.reshape(hshape), mv.reshape(hshape),
                         mi + gate_b[:M_HEADS], mf + gate_b[M_HEADS:])
    hm = head_rms_norm(hm, m_norm_w) * jax.nn.sigmoid(mo.astype(f32)).reshape(hshape)
    hm = hm.astype(u.dtype).reshape(B, S, M_W)

    aq = partial_rope(aq.reshape(B, S, A_HEADS, 2, A_DH), cos, sin)
    ak = partial_rope(ak.reshape(B, S, A_HEADS, 2, A_DH), cos, sin)
    lam_init = 0.8 - 0.6 * math.exp(-0.3 * layer_idx)
    lv = lam_vecs.astype(f32)
    lam = jnp.exp(jnp.sum(lv[0] * lv[1])) - jnp.exp(jnp.sum(lv[2] * lv[3])) + lam_init
    ha = diff_attention(aq, ak, av.reshape(B, S, A_HEADS, 2 * A_DH), lam)
    ha = (head_rms_norm(ha, a_norm_w) * (1.0 - lam_init)).astype(u.dtype).reshape(B, S, A_W)

    f_g = lb + (1.0 - lb) * jax.nn.sigmoid(gf.astype(f32))
    log_f = jnp.log(jnp.maximum(f_g, F_FLOOR))
    k_g = 1.0 - f_g
    kshape = (B, S, G_HEADS, G_DK)
    hg = hgrn2_chunkwise(jax.nn.silu(gq).reshape(kshape), k_g.reshape(kshape),
                         gi.reshape(B, S, G_HEADS, G_DV), log_f.reshape(kshape))
    hg = head_rms_norm(hg, g_norm_w) * jax.nn.silu(gg.astype(f32)).reshape(B, S, G_HEADS, G_DV)
    hg = hg.astype(u.dtype).reshape(B, S, G_WV)

    gates = jax.nn.sigmoid(gate_pre).reshape(B, S, N_BRANCH, D_MODEL)
    y = (gates[:, :, 0] * (hm @ p_m) + gates[:, :, 1] * (ha @ p_a) + gates[:, :, 2] * (hg @ p_g))
    return y @ w_out


def squared_relu_mlp(u, w_up, w_down):
    return jnp.square(jax.nn.relu(u @ w_up)) @ w_down


def setup_inputs(seed: int = 0) -> dict:
    key = jax.random.key(seed)
    ks = iter(jax.random.split(key, 32))
    L = DEPTH

    def nrm(shape, scale):
        return scale * jax.random.normal(next(ks), shape, jnp.float32)

    x = nrm((BATCH, SEQ, D_MODEL), 1.0)
    positions = jnp.broadcast_to(jnp.arange(SEQ, dtype=jnp.int32), (BATCH, SEQ))
    ln0_w = 1.0 + nrm((D_MODEL,), 0.02)
    ln0_b = nrm((D_MODEL,), 0.02)
    w_in = nrm((L, D_MODEL, IN_WIDTH), D_MODEL ** -0.5)
    m_conv_w = nrm((L, CONV_K, 2 * M_W), CONV_K ** -0.5)
    m_conv_b = nrm((L, 2 * M_W), 0.02)
    f_bias = jnp.linspace(3.0, 6.0, M_HEADS, dtype=jnp.float32)
    m_gate_b = jnp.concatenate([nrm((L, M_HEADS), 0.1), f_bias + nrm((L, M_HEADS), 0.1)], axis=-1)
    m_norm_w = 1.0 + nrm((L, M_W), 0.02)
    a_lambda = nrm((L, 4, A_DH), 0.1)
    a_norm_w = 1.0 + nrm((L, A_W), 0.02)
    g_lb_logits = nrm((L, G_WK), 0.5)
    g_norm_w = 1.0 + nrm((L, G_WV), 0.02)
    p_m = nrm((L, M_W, D_MODEL), M_W ** -0.5)
    p_a = nrm((L, A_W, D_MODEL), A_W ** -0.5)
    p_g = nrm((L, G_WV, D_MODEL), G_WV ** -0.5)
    w_out = nrm((L, D_MODEL, D_MODEL), BETA * D_MODEL ** -0.5)
    ln1_w = 1.0 + nrm((L, D_MODEL), 0.02)
    ln1_b = nrm((L, D_MODEL), 0.02)
    w_up = nrm((L, D_MODEL, D_FF), D_MODEL ** -0.5)
    w_down = nrm((L, D_FF, D_MODEL), BETA * D_FF ** -0.5)
    ln2_w = 1.0 + nrm((L, D_MODEL), 0.02)
    ln2_b = nrm((L, D_MODEL), 0.02)
    return {'x': x, 'positions': positions, 'ln0_w': ln0_w, 'ln0_b': ln0_b, 'w_in': w_in,
            'm_conv_w': m_conv_w, 'm_conv_b': m_conv_b, 'm_gate_b': m_gate_b, 'm_norm_w': m_norm_w,
            'a_lambda': a_lambda, 'a_norm_w': a_norm_w, 'g_lb_logits': g_lb_logits,
            'g_norm_w': g_norm_w, 'p_m': p_m, 'p_a': p_a, 'p_g': p_g, 'w_out': w_out,
            'ln1_w': ln1_w, 'ln1_b': ln1_b, 'w_up': w_up, 'w_down': w_down,
            'ln2_w': ln2_w, 'ln2_b': ln2_b}


def reference(x, positions, ln0_w, ln0_b, w_in, m_conv_w, m_conv_b, m_gate_b, m_norm_w,
              a_lambda, a_norm_w, g_lb_logits, g_norm_w, p_m, p_a, p_g, w_out,
              ln1_w, ln1_b, w_up, w_down, ln2_w, ln2_b):
    cos, sin = rope_cos_sin(positions)
    p_lb = jax.nn.softmax(g_lb_logits.astype(jnp.float32), axis=0)
    lower_bounds = jnp.cumsum(p_lb, axis=0) - p_lb[0]
    h = layer_norm(x, ln0_w, ln0_b)
    for l in range(DEPTH):
        mix = mixer_sublayer(h, l, cos, sin, lower_bounds[l], w_in[l], m_conv_w[l], m_conv_b[l],
                             m_gate_b[l], m_norm_w[l], a_lambda[l], a_norm_w[l], g_norm_w[l],
                             p_m[l], p_a[l], p_g[l], w_out[l])
        h = layer_norm(ALPHA * h + mix, ln1_w[l], ln1_b[l])
        ff = squared_relu_mlp(h, w_up[l], w_down[l])
        h = layer_norm(ALPHA * h + ff, ln2_w[l], ln2_b[l])
    return h
```

```python
import math
import contextlib
import numpy as np
import concourse.bass as bass
import concourse.mybir as mybir
from concourse.bass_utils import run_bass_kernel_spmd

F32 = mybir.dt.float32
BF16 = mybir.dt.bfloat16
I32 = mybir.dt.int32
AF = mybir.ActivationFunctionType
ALU = mybir.AluOpType

D = 2048
S = 2048
NT = 16
KC = 16
DEPTH = 2
M_H, M_DH, M_W = 4, 256, 1024
A_H, A_DH, A_W = 8, 64, 1024
G_H, G_DK, G_W = 8, 128, 1024
D_FF = 8192
ALPHA = (2 * DEPTH) ** 0.25
LN_EPS = 1e-5
NORM_EPS = 1e-6
ROPE_THETA = 500000.0
IN_SIZES = (M_W, M_W, M_W, M_W, M_H, M_H, A_W, A_W, A_W, G_W, G_W, G_W, G_W, 3 * D)
OFFS = [0]
for _s in IN_SIZES:
    OFFS.append(OFFS[-1] + _s)
(O_MQ, O_MK, O_MV, O_MO, O_MI, O_MF, O_AQ, O_AK, O_AV, O_GQ, O_GF, O_GI, O_GG, O_GATE) = OFFS[:14]
IN_W = OFFS[14]
LN16 = math.log(1.0 / 16.0)

N_DMA_SEM = 30


class Buf:
    __slots__ = ("last_w", "readers", "excl")

    def __init__(self, excl=False):
        self.last_w = None
        self.readers = []
        self.excl = excl


class Op:
    __slots__ = ("eng", "fn", "deps", "is_dma", "inc", "val", "sem_i", "prev_val", "bar")

    def __init__(self, eng, fn, is_dma):
        self.eng = eng
        self.fn = fn
        self.deps = []
        self.is_dma = is_dma
        self.inc = False
        self.val = 0
        self.sem_i = -1
        self.prev_val = 0
        self.bar = False


class Rec:
    ENGS = ("pe", "act", "dve", "pool", "sp")
    DMA_SHARE = {"sp": list(range(0, 16)), "pool": list(range(16, 28)), "act": list(range(28, 30))}

    def __init__(self):
        self.ops = {e: [] for e in self.ENGS}
        self.dma_j = {e: 0 for e in self.ENGS}
        self.dma_cnt = [0] * N_DMA_SEM
        self.last_dma = [None] * N_DMA_SEM
        self.last_comp = {e: None for e in self.ENGS}

    def op(self, eng, fn, reads=(), writes=(), dma=False):
        o = Op(eng, fn, dma)
        deps = []
        for b in reads:
            if b.last_w is not None:
                deps.append((b.last_w, True))
            if b.excl:
                for r in b.readers:
                    if r.eng != eng:
                        deps.append((r, True))
        for b in writes:
            if b.last_w is not None:
                deps.append((b.last_w, True))
            for r in b.readers:
                deps.append((r, False))
        for b in reads:
            b.readers.append(o)
        for b in writes:
            b.last_w = o
            b.readers = []
        seen = set()
        for d, is_w in deps:
            if d is o or id(d) in seen:
                continue
            if (not d.is_dma) and (not dma) and d.eng == eng:
                if eng == "pe" or not is_w:
                    continue
            seen.add(id(d))
            o.deps.append(d)
            d.inc = True
        if dma:
            sh = self.DMA_SHARE[eng]
            si = sh[self.dma_j[eng] % len(sh)]
            self.dma_j[eng] += 1
            o.sem_i = si
            o.prev_val = self.dma_cnt[si]
            self.dma_cnt[si] += 16
            o.val = self.dma_cnt[si]
            self.last_dma[si] = o
        else:
            self.last_comp[eng] = o
        self.ops[eng].append(o)
        return o

    def barrier(self):
        o = Op("sp", lambda e: e.nop(), False)
        for e in self.ENGS:
            d = self.last_comp[e]
            if d is not None and e != "sp":
                d.inc = True
                o.deps.append(d)
        for d in self.last_dma:
            if d is not None:
                o.deps.append(d)
        o.inc = True
        self.ops["sp"].append(o)
        self.last_comp["sp"] = o
        for e in self.ENGS:
            if e == "sp":
                continue
            n = Op(e, lambda eo: eo.nop(), False)
            n.deps.append(o)
            self.ops[e].append(n)
            self.last_comp[e] = n
        return o

    def emit(self, nc, sems, dma_sems):
        engobj = {"pe": nc.tensor, "act": nc.scalar, "dve": nc.vector,
                  "pool": nc.gpsimd, "sp": nc.sync}
        for e in self.ENGS:
            c = 0
            for o in self.ops[e]:
                if (not o.is_dma) and o.inc:
                    c += 1
                    o.val = c
        n_wait = 0
        n_ins = 0
        for e in self.ENGS:
            eo = engobj[e]
            waited = {}
            for o in self.ops[e]:
                need = {}
                for d in o.deps:
                    key = ("d", d.sem_i) if d.is_dma else ("e", d.eng)
                    if d.val > need.get(key, 0):
                        need[key] = d.val
                if o.is_dma and o.prev_val > 0:
                    key = ("d", o.sem_i)
                    if o.prev_val > need.get(key, 0):
                        need[key] = o.prev_val
                for key, v in need.items():
                    if waited.get(key, 0) >= v:
                        continue
                    waited[key] = v
                    s = dma_sems[key[1]] if key[0] == "d" else sems[key[1]]
                    eo.wait_ge(s, v)
                    n_wait += 1
                ins = o.fn(eo)
                n_ins += 1
                if o.is_dma:
                    ins.then_inc(dma_sems[o.sem_i], 16)
                elif o.inc:
                    ins.then_inc(sems[e], 1)
        return n_ins, n_wait


class TL:
    def __init__(self, t, excl=False):
        self.t = t
        self.excl = excl
        self.b = Buf(excl)
        self.subs = {}

    def sub(self, key):
        s = self.subs.get(key)
        if s is None:
            s = Buf(self.excl)
            self.subs[key] = s
        return s

    def __getitem__(self, k):
        return self.t[k]


def _bl(x):
    out = []
    for v in x:
        if v is None:
            continue
        out.append(v.b if isinstance(v, TL) else v)
    return out


class Stage:
    def __init__(self, k):
        self.k = k
        self.es = contextlib.ExitStack()

    def __enter__(self):
        self.es.__enter__()
        return self

    def __exit__(self, *a):
        self.k.R.barrier()
        return self.es.__exit__(*a)

    def sb(self, name, shape, dt=F32):
        self.k.uid += 1
        return TL(self.es.enter_context(self.k.nc.sbuf_tensor("%s_%d" % (name, self.k.uid), list(shape), dt)))

    def ps(self, name, shape, dt=F32):
        self.k.uid += 1
        nbytes = int(np.prod(shape[1:])) * (2 if dt == BF16 else 4)
        assert nbytes % 2048 == 0, (name, shape)
        return TL(self.es.enter_context(self.k.nc.psum_tensor("%s_%d" % (name, self.k.uid), list(shape), dt)), excl=True)


CONST_NAMES = ["ident", "maskA", "rotT", "invf", "neg64", "M1", "Mref", "maskG", "blk2",
               "sel0", "sel1", "sel2", "sel3", "neg128"]

def host_consts():
    c = {}
    c["ident"] = np.eye(128, dtype=np.float32)
    k = np.arange(128)[:, None]
    q = np.arange(128)[None, :]
    c["maskA"] = (k <= q).astype(np.float32)
    rot = np.zeros((128, 128), np.float32)
    invf = np.zeros((128, 1), np.float32)
    inv = ROPE_THETA ** (-np.arange(0, 16, 2, dtype=np.float32) / 16.0)
    for p in range(128):
        d = p % 64
        if d < 8:
            rot[p + 8, p] = -1.0
            invf[p, 0] = inv[d]
        elif d < 16:
            rot[p - 8, p] = 1.0
            invf[p, 0] = inv[d - 8]
    c["rotT"] = rot
    c["invf"] = np.repeat(invf, 128, axis=1)
    s = np.arange(64)[:, None]
    t = np.arange(64)[None, :]
    neg = np.where(s > t, -30000.0, 0.0).astype(np.float32)
    c["neg64"] = np.pad(neg, ((0, 64), (0, 64)))
    ch = np.arange(128) // 64
    same = ch[:, None] == ch[None, :]
    Lm = (same & (k <= q)).astype(np.float32)
    ref = ch * 64 + 31
    c["M1"] = Lm - Lm[:, ref]
    mref = np.zeros((128, 128), np.float32)
    mref[:, 0] = Lm[:, 31]
    mref[:, 1] = Lm[:, 63] - Lm[:, 31]
    mref[:, 2] = Lm[:, 95]
    mref[:, 3] = Lm[:, 127] - Lm[:, 95]
    mref[:, 4] = Lm[:, 63]
    mref[:, 5] = Lm[:, 127]
    c["Mref"] = mref
    c["maskG"] = Lm
    blk = np.zeros((128, 128), np.float32)
    blk[:64, 0] = 1.0
    blk[64:, 1] = 1.0
    c["blk2"] = blk
    for hsel in range(4):
        sl = np.zeros((128, 128), np.float32)
        sl[hsel, :] = 1.0
        c["sel%d" % hsel] = sl
    c["neg128"] = ((Lm - 1.0) * 30000.0).astype(np.float32)
    names = list(CONST_NAMES)
    pack = np.concatenate([c[n] for n in names], axis=1).astype(np.float32)
    tt = np.arange(S)
    crow = np.zeros((1, 2 * S), np.float32)
    crow[0, :S] = (tt % 64 != 0).astype(np.float32)
    crow[0, S:] = ((tt % 64 == 0) & (tt > 0)).astype(np.float32)
    return names, pack, crow


class K:
    def __init__(self, nc, dbg=False):
        self.nc = nc
        self.R = Rec()
        self.uid = 0
        self.dbg = dbg
        L = DEPTH
        self.decl = {
            "x": ([S, D], F32), "positions": ([1, S], I32), "ln0_w": ([D], F32), "ln0_b": ([D], F32),
            "w_in": ([L, D, IN_W], F32), "m_conv_w": ([L, 4, 2 * M_W], F32), "m_conv_b": ([L, 2 * M_W], F32),
            "m_gate_b": ([L, 2 * M_H], F32), "m_norm_w": ([L, M_W], F32), "a_lambda": ([L, 4 * A_DH], F32),
            "a_norm_w": ([L, A_W], F32), "g_lb_logits": ([L, G_W], F32), "g_norm_w": ([L, G_W], F32),
            "p_m": ([L, M_W, D], F32), "p_a": ([L, A_W, D], F32), "p_g": ([L, G_W, D], F32),
            "w_out": ([L, D, D], F32), "ln1_w": ([L, D], F32), "ln1_b": ([L, D], F32),
            "w_up": ([L, D, D_FF], F32), "w_down": ([L, D_FF, D], F32), "ln2_w": ([L, D], F32),
            "ln2_b": ([L, D], F32), "consts": ([128, 128 * len(CONST_NAMES)], F32), "crow": ([1, 2 * S], F32),
        }
        self.declared = {}
        self.out = nc.dram_tensor("out", [S, D], F32, kind="ExternalOutput").ap()
        kind = "ExternalOutput" if dbg else "Internal"
        g = lambda name, shape, dt: nc.dram_tensor(name, list(shape), dt, kind=kind).ap()
        self.hres = g("hres", [S, D], F32)
        self.hT_d = g("hT_d", [D, S], BF16)
        self.mixT_d = g("mixT_d", [3 * 1024, S], BF16)
        self.yT_d = g("yT_d", [D, S], BF16)
        self.gsc = g("gsc", [8, S], F32)

    def __getattr__(self, name):
        decl = self.__dict__.get("decl", {})
        key = {"pos": "positions", "g_lb": "g_lb_logits"}.get(name, name)
        if key in decl:
            if key not in self.declared:
                shape, dt = decl[key]
                self.declared[key] = self.nc.dram_tensor(key, list(shape), dt, kind="ExternalInput").ap()
            return self.declared[key]
        raise AttributeError(name)

    def mm(self, out, lhsT, rhs, start, stop, r, w, skip=False):
        if skip:
            fn = lambda e: e.matmul(out, lhsT=lhsT, rhs=rhs, start=start, stop=stop, skip_group_check=True)
        else:
            fn = lambda e: e.matmul(out, lhsT=lhsT, rhs=rhs, start=start, stop=stop)
        return self.R.op("pe", fn, _bl(r), _bl(w))

    def tr(self, out, in_, ident, r, w):
        return self.R.op("pe", lambda e: e.transpose(out, in_, ident), _bl(r), _bl(w))

    def act(self, out, in_, func, r, w, bias=None, scale=None, accum=None):
        kw = {}
        if bias is not None:
            kw["bias"] = bias
        if scale is not None:
            kw["scale"] = scale
        if accum is not None:
            kw["accum_out"] = accum
        return self.R.op("act", lambda e: e.activation(out=out, in_=in_, func=func, **kw), _bl(r), _bl(w))

    def tt(self, eng, out, in0, in1, op, r, w):
        return self.R.op(eng, lambda e: e.tensor_tensor(out=out, in0=in0, in1=in1, op=op), _bl(r), _bl(w))

    def ts(self, eng, out, in0, s1, op0, r, w, s2=None, op1=None, accum=None):
        kw = {}
        if op1 is not None:
            kw["op1"] = op1
        if accum is not None:
            kw["accum_out"] = accum
        return self.R.op(eng, lambda e: e.tensor_scalar(out=out, in0=in0, scalar1=s1, scalar2=s2, op0=op0, **kw),
                         _bl(r), _bl(w))

    def stt(self, eng, out, in0, scalar, in1, op0, op1, r, w):
        return self.R.op(eng, lambda e: e.scalar_tensor_tensor(out=out, in0=in0, scalar=scalar, in1=in1,
                                                               op0=op0, op1=op1), _bl(r), _bl(w))

    def cp(self, eng, out, in_, r, w):
        return self.R.op(eng, lambda e: e.tensor_copy(out=out, in_=in_), _bl(r), _bl(w))

    def ms(self, eng, ap, val, w):
        return self.R.op(eng, lambda e: e.memset(ap, val), [], _bl(w))

    def recip(self, out, in_, r, w):
        return self.R.op("dve", lambda e: e.reciprocal(out=out, in_=in_), _bl(r), _bl(w))

    def rmax(self, out, in_, r, w):
        return self.R.op("dve", lambda e: e.reduce_max(out=out, in_=in_, axis=mybir.AxisListType.X), _bl(r), _bl(w))

    def rsum(self, out, in_, r, w):
        return self.R.op("dve", lambda e: e.reduce_sum(out=out, in_=in_, axis=mybir.AxisListType.X), _bl(r), _bl(w))

    def scan(self, out, d0, d1, init, op0, op1, r, w):
        return self.R.op("dve", lambda e: e.tensor_tensor_scan(out=out, data0=d0, data1=d1, initial=init,
                                                               op0=op0, op1=op1), _bl(r), _bl(w))

    def dma(self, eng, out, in_, r, w):
        return self.R.op(eng, lambda e: e.dma_start(out=out, in_=in_), _bl(r), _bl(w), dma=True)

    def dump(self, name, ap, shape, deps, dt=F32):
        if not self.dbg:
            return
        d = self.nc.dram_tensor("dbg_" + name, list(shape), dt, kind="ExternalOutput").ap()
        self.dma("sp", d, ap, deps, [])

    def setup_consts(self, es):
        nc = self.nc

        def sb(name, shape, dt=F32):
            return TL(es.enter_context(nc.sbuf_tensor(name, list(shape), dt)))
        ncst = len(CONST_NAMES)
        self.cst = sb("cst", [128, ncst * 128])
        self.dma("sp", self.cst[:], self.consts[:, :], [], [self.cst])
        self.c = {n: self.cst.t[:, i * 128:(i + 1) * 128] for i, n in enumerate(CONST_NAMES)}
        self.cstb = sb("cstb", [128, ncst * 128], BF16)
        self.cp("dve", self.cstb[:], self.cst[:], [self.cst], [self.cstb])
        self.cb = {n: self.cstb.t[:, i * 128:(i + 1) * 128] for i, n in enumerate(CONST_NAMES)}
        self.ones_f = sb("ones_f", [128, 128])
        self.ms("dve", self.ones_f[:], 1.0, [self.ones_f])
        self.ones_b = sb("ones_b", [128, 128], BF16)
        self.ms("dve", self.ones_b[:], 1.0, [self.ones_b])
        self.CR = [self.cst, self.cstb, self.ones_f, self.ones_b]

    def ln_alloc(self, st):
        a = {}
        a["stats"] = [st.sb("lnst", [128, 4, 6]) for _ in range(2)]
        a["mv"] = [st.sb("lnmv", [128, 2]) for _ in range(2)]
        a["sd"] = [st.sb("lnsd", [128, 4]) for _ in range(2)]
        a["xn"] = st.sb("lnxn", [128, D])
        a["hh"] = [st.sb("lnhh", [128, D]) for _ in range(2)]
        a["hb"] = st.sb("lnhb", [128, D], BF16)
        a["hTs"] = [st.sb("lnhTs", [128, KC, 512], BF16) for _ in range(2)]
        a["pt"] = [st.ps("lnpt", [128, 8, 128], BF16) for _ in range(2)]
        a["cnt"] = 0
        return a

    def ln_block(self, a, z, zr, lnw, lnb, tb, res_out, hT_out=True):
        i = a["cnt"] % 2
        a["cnt"] += 1
        stats, mv, sd = a["stats"][i], a["mv"][i], a["sd"][i]
        for j in range(4):
            self.R.op("dve", (lambda e, j=j: e.bn_stats(out=stats.t[:, j, :], in_=z[:, j * 512:(j + 1) * 512])),
                      _bl(zr), _bl([stats]))
        self.R.op("dve", lambda e: e.bn_aggr(out=mv.t[:, :], in_=stats.t[:, :, :]), _bl([stats]), _bl([mv]))
        self.ts("dve", sd.t[:, 3:4], mv.t[:, 1:2], LN_EPS, ALU.add, [mv], [sd])
        self.act(sd.t[:, 0:1], sd.t[:, 3:4], AF.Sqrt, [sd], [sd])
        self.recip(sd.t[:, 1:2], sd.t[:, 0:1], [sd], [sd])
        self.stt("dve", sd.t[:, 2:3], mv.t[:, 0:1], -1.0, sd.t[:, 1:2], ALU.mult, ALU.mult, [mv, sd], [sd])
        xn = a["xn"]
        self.act(xn.t[:, :], z, AF.Identity, list(zr) + [sd], [xn], bias=sd.t[:, 2:3], scale=sd.t[:, 1:2])
        self.tt("dve", xn.t[:, :], xn.t[:, :], lnw.t[:, :], ALU.mult, [xn, lnw], [xn])
        hh = a["hh"][i]
        self.tt("pool", hh.t[:, :], xn.t[:, :], lnb.t[:, :], ALU.add, [xn, lnb], [hh])
        self.dma("sp", res_out[tb * 128:(tb + 1) * 128, :], hh.t[:, :], [hh], [])
        if not hT_out:
            return
        hb = a["hb"]
        self.act(hb.t[:, :], hh.t[:, :], AF.Copy, [hh], [hb])
        g4 = tb // 4
        hTs = a["hTs"][g4 % 2]
        for g in range(2):
            pt = a["pt"][g]
            for j in range(8):
                kc = g * 8 + j
                self.tr(pt.t[:, j, :], hb.t[:, kc * 128:(kc + 1) * 128], self.cb["ident"], [hb] + self.CR, [pt])
            eng = "dve" if g == 0 else "act"
            if eng == "dve":
                self.cp("dve", hTs.t[:, g * 8:(g + 1) * 8, (tb % 4) * 128:(tb % 4 + 1) * 128], pt.t[:, :, :], [pt], [hTs])
            else:
                self.act(hTs.t[:, g * 8:(g + 1) * 8, (tb % 4) * 128:(tb % 4 + 1) * 128], pt.t[:, :, :], AF.Copy, [pt], [hTs])
        if tb % 4 == 3:
            dst = self.hT_d.rearrange("(kc p) t -> p kc t", p=128)[:, :, g4 * 512:(g4 + 1) * 512]
            self.dma("sp", dst, hTs.t[:, :, :], [hTs], [])

    def load_bc(self, st, name, src_row, n):
        t = st.sb(name, [128, n])
        self.dma("sp", t.t[:, :], src_row.partition_broadcast(128), [], [t])
        return t

    def stage_ln0(self):
        with Stage(self) as st:
            lnw = self.load_bc(st, "lnw", self.ln0_w, D)
            lnb = self.load_bc(st, "lnb", self.ln0_b, D)
            a = self.ln_alloc(st)
            xin = [st.sb("xin", [128, D]) for _ in range(2)]
            for tb in range(NT):
                z = xin[tb % 2]
                self.dma("sp", z.t[:, :], self.x[tb * 128:(tb + 1) * 128, :], [], [z])
                self.ln_block(a, z.t[:, :], [z], lnw, lnb, tb, self.hres)

    def load_hT(self, st):
        hT = st.sb("hT", [128, KC, S], BF16)
        for kc in range(KC):
            self.dma("sp", hT.t[:, kc, :], self.hT_d[kc * 128:(kc + 1) * 128, :], [], [hT.sub(kc)])
        return hT

    def sin_table(self, st, out, ang, shift, tmp, kk):
        two_pi = 2.0 * math.pi
        self.ts("dve", tmp.t[:, :], ang.t[:, :], 1.0 / two_pi, ALU.mult, [ang], [tmp], s2=shift / two_pi, op1=ALU.add)
        self.cp("dve", kk.t[:, :], tmp.t[:, :], [tmp], [kk])
        self.cp("dve", tmp.t[:, :], kk.t[:, :], [kk], [tmp])
        self.stt("dve", tmp.t[:, :], tmp.t[:, :], -two_pi, ang.t[:, :], ALU.mult, ALU.add, [tmp, ang], [tmp])
        self.ts("dve", tmp.t[:, :], tmp.t[:, :], shift, ALU.add, [tmp], [tmp], s2=-math.pi, op1=ALU.max)
        self.ts("dve", tmp.t[:, :], tmp.t[:, :], math.pi, ALU.min, [tmp], [tmp])
        self.act(out.t[:, :], tmp.t[:, :], AF.Sin, [tmp], [out])

    def stage_attn(self, l):
        lam_init = 0.8 - 0.6 * math.exp(-0.3 * l)
        with Stage(self) as st:
            CR = self.CR
            hT = self.load_hT(st)
            Cs = st.sb("Cs", [128, S])
            Sn = st.sb("Sn", [128, S])
            with Stage(self) as s2:
                wif = s2.sb("wif", [128, KC, 8], BF16)
                self.dma("pool", wif.t[:, :, :], self.w_in[l][:, O_MI:O_MI + 8].rearrange("(kc p) c -> p kc c", p=128), [], [wif])
                g4 = s2.sb("g4", [4, 2, S])
                psG = s2.ps("psG", [128, 512])
                for which in range(2):
                    for tg in range(4):
                        for kc in range(KC):
                            self.mm(psG.t[0:4, :], wif.t[:, kc, which * 4:(which + 1) * 4], hT.t[:, kc, tg * 512:(tg + 1) * 512],
                                    kc == 0, kc == KC - 1, [wif, hT.sub(kc)], [psG])
                        self.act(g4.t[0:4, which, tg * 512:(tg + 1) * 512], psG.t[0:4, :], AF.Copy, [psG], [g4])
                self.dma("sp", self.gsc.rearrange("(w h) t -> h w t", h=4), g4.t[0:4, :, :], [g4], [])
                posi = s2.sb("posi", [128, S], I32)
                self.dma("sp", posi.t[:, :], self.pos[0].partition_broadcast(128), [], [posi])
                ang = s2.sb("ang", [128, S])
                tmpf = s2.sb("tmpf", [128, S])
                self.cp("dve", ang.t[:, :], posi.t[:, :], [posi], [ang])
                self.ts("dve", ang.t[:, :], ang.t[:, :], self.c["invf"][:, 0:1], ALU.mult, [ang] + CR, [ang])
                self.sin_table(s2, Sn, ang, 0.0, tmpf, posi)
                self.sin_table(s2, Cs, ang, math.pi / 2.0, tmpf, posi)
            lv = self.load_bc(st, "lv", self.a_lambda[l], 4 * A_DH)
            lsm = st.sb("lsm", [128, 8])
            lpr = st.sb("lpr", [128, 2 * A_DH])
            self.tt("dve", lpr.t[:, 0:64], lv.t[:, 0:64], lv.t[:, 64:128], ALU.mult, [lv], [lpr])
            self.tt("dve", lpr.t[:, 64:128], lv.t[:, 128:192], lv.t[:, 192:256], ALU.mult, [lv], [lpr])
            self.rsum(lsm.t[:, 0:1], lpr.t[:, 0:64], [lpr], [lsm])
            self.rsum(lsm.t[:, 1:2], lpr.t[:, 64:128], [lpr], [lsm])
            self.act(lsm.t[:, 2:4], lsm.t[:, 0:2], AF.Exp, [lsm], [lsm])
            self.tt("dve", lsm.t[:, 4:5], lsm.t[:, 3:4], lsm.t[:, 2:3], ALU.subtract, [lsm], [lsm])
            self.ts("dve", lsm.t[:, 5:6], lsm.t[:, 4:5], -lam_init, ALU.add, [lsm], [lsm])
            nlam = lsm.t[:, 5:6]
            anw = self.load_bc(st, "anw", self.a_norm_w[l], A_W)
            self.ts("dve", anw.t[:, :], anw.t[:, :], 1.0 - lam_init, ALU.mult, [anw], [anw])
            wts = [st.sb("awt", [128, KC, 384], BF16) for _ in range(2)]
            qT = [st.sb("aqT", [128, S], BF16) for _ in range(2)]
            kT = [st.sb("akT", [128, S], BF16) for _ in range(2)]
            va = [st.sb("ava", [128, NT, 132], BF16) for _ in range(2)]
            for p_ in range(2):
                self.ms("dve", va[p_].t[:, :, 128:132], 1.0, [va[p_]])
            sq = st.sb("asq", [128, S], BF16)
            qsb = [st.sb("aqs", [128, 512], BF16) for _ in range(2)]
            t1s = [st.sb("at1", [128, 512]) for _ in range(2)]
            t2s = [st.sb("at2", [128, 512]) for _ in range(2)]
            pTs = [st.sb("apT", [128, 512], BF16) for _ in range(3)]
            mx = st.sb("amx", [128, 16])
            dg = st.sb("adg", [128, 2])
            negm = [st.sb("anegm", [128, 2]) for _ in range(2)]
            tmp1 = st.sb("atmp1", [128, 4, 128])
            res = st.sb("ares", [128, 4, 128])
            rr = st.sb("arr", [128, 16])
            junk = st.sb("ajunk", [128, 128])
            hab = st.sb("ahab", [128, 4, 128], BF16)
            haT = [st.sb("ahaT", [128, S], BF16) for _ in range(2)]
            psA = [st.ps("apsA", [128, 512]) for _ in range(2)]
            psR = st.ps("apsR", [128, 512])
            psS = [st.ps("apsS", [128, 512]) for _ in range(2)]
            psO = st.ps("apsO", [128, 4, 256])
            psT = st.ps("apsT", [128, 8, 128], BF16)
            cA = 0
            cS = 0
            for h in range(A_H):
                p_ = h % 2
                w_ = wts[p_]
                for j, off in enumerate((O_AQ, O_AK, O_AV)):
                    src = self.w_in[l][:, off + h * 128: off + (h + 1) * 128].rearrange("(kc p) c -> p kc c", p=128)
                    self.dma("pool", w_.t[:, :, j * 128:(j + 1) * 128], src, [], [w_.sub(j)])
                for which, dst in enumerate((qT[p_], kT[p_])):
                    for tg in range(4):
                        pa = psA[cA % 2]
                        qs, t1, t2 = qsb[cA % 2], t1s[cA % 2], t2s[cA % 2]
                        cA += 1
                        for kc in range(KC):
                            self.mm(pa.t[:, :], w_.t[:, kc, which * 128:(which + 1) * 128],
                                    hT.t[:, kc, tg * 512:(tg + 1) * 512], kc == 0, kc == KC - 1,
                                    [w_.sub(which), hT.sub(kc)], [pa])
                        self.act(qs.t[:, :], pa.t[:, :], AF.Copy, [pa], [qs])
                        self.mm(psR.t[:, :], self.cb["rotT"], qs.t[:, :], True, True, [qs] + CR, [psR])
                        self.tt("dve", t1.t[:, :], qs.t[:, :], Cs.t[:, tg * 512:(tg + 1) * 512], ALU.mult, [qs, Cs], [t1])
                        self.tt("dve", t2.t[:, :], psR.t[:, :], Sn.t[:, tg * 512:(tg + 1) * 512], ALU.mult, [psR, Sn], [t2])
                        self.tt("pool", dst.t[:, tg * 512:(tg + 1) * 512], t1.t[:, :], t2.t[:, :], ALU.add, [t1, t2], [dst.sub(tg)])
                for tb in range(NT):
                    pa = psA[cA % 2]
                    cA += 1
                    for kc in range(KC):
                        self.mm(pa.t[:, 0:128], hT.t[:, kc, tb * 128:(tb + 1) * 128], w_.t[:, kc, 256:384],
                                kc == 0, kc == KC - 1, [w_.sub(2), hT.sub(kc)], [pa])
                    self.act(va[p_].t[:, tb, 0:128], pa.t[:, 0:128], AF.Copy, [pa], [va[p_].sub(tb)])
                for which, src in enumerate((qT[p_], kT[p_])):
                    self.tt("dve", sq.t[:, :], src.t[:, :], src.t[:, :], ALU.mult, [src.sub(t_) for t_ in range(4)], [sq])
                    for tg in range(4):
                        pa = psA[cA % 2]
                        cA += 1
                        self.mm(pa.t[0:2, :], self.cb["blk2"][:, 0:2], sq.t[:, tg * 512:(tg + 1) * 512], True, True, [sq] + CR, [pa])
                        self.rmax(mx.t[0:2, which * 4 + tg: which * 4 + tg + 1], pa.t[0:2, :], [pa], [mx])
                self.rmax(mx.t[0:2, 8:9], mx.t[0:2, 0:4], [mx], [mx])
                self.rmax(mx.t[0:2, 9:10], mx.t[0:2, 4:8], [mx], [mx])
                self.tt("dve", mx.t[0:2, 10:11], mx.t[0:2, 8:9], mx.t[0:2, 9:10], ALU.mult, [mx], [mx])
                self.act(mx.t[0:2, 11:12], mx.t[0:2, 10:11], AF.Sqrt, [mx], [mx])
                self.ts("dve", mx.t[0:2, 12:13], mx.t[0:2, 11:12], -0.125, ALU.mult, [mx], [mx])
                self.ts("dve", dg.t[0:2, 0:2], self.c["ident"][0:2, 0:2], mx.t[0:2, 12:13], ALU.mult, [mx] + CR, [dg])
                pa = psA[cA % 2]
                cA += 1
                self.mm(pa.t[:, 0:2], self.ones_f.t[0:2, :], dg.t[0:2, 0:2], True, True, [dg] + CR, [pa])
                ng = negm[p_]
                self.cp("dve", ng.t[:, :], pa.t[:, 0:2], [pa], [ng])
                qd = [qT[p_].sub(t_) for t_ in range(4)]
                kd = [kT[p_].sub(t_) for t_ in range(4)]
                for g in range(4):
                    for n in range(2):
                        self.ms("dve", psO.t[:, :, :], 0.0, [psO])
                        for j in range(4 * g + 4):
                            q0 = max(j * 128, g * 512)
                            qn = (g + 1) * 512 - q0
                            pst = psS[cS % 2]
                            pT = pTs[cS % 3]
                            cS += 1
                            self.mm(pst.t[:, 0:qn], kT[p_].t[n * 64:(n + 1) * 64, j * 128:(j + 1) * 128],
                                    qT[p_].t[n * 64:(n + 1) * 64, q0:q0 + qn], True, True, qd + kd, [pst])
                            self.act(pT.t[:, 0:qn], pst.t[:, 0:qn], AF.Exp, [pst, ng], [pT],
                                     bias=ng.t[:, n:n + 1], scale=0.125)
                            if j * 128 >= g * 512:
                                self.tt("pool", pT.t[:, 0:128], pT.t[:, 0:128], self.cb["maskA"], ALU.mult, [pT] + CR, [pT])
                            for b in range(4):
                                if g * 4 + b < j:
                                    continue
                                c0 = g * 512 + b * 128 - q0
                                self.mm(psO.t[:, b, 0:129], pT.t[:, c0:c0 + 128], va[p_].t[:, j, 0:129],
                                        False, False, [pT, va[p_].sub(j)], [psO], skip=True)
                        if n == 0:
                            self.recip(rr.t[:, 0:4], psO.t[:, :, 128], [psO], [rr])
                            for b in range(4):
                                self.act(tmp1.t[:, b, :], psO.t[:, b, 0:128], AF.Copy, [psO, rr], [tmp1], scale=rr.t[:, b:b + 1])
                        else:
                            self.recip(rr.t[:, 4:8], psO.t[:, :, 128], [psO], [rr])
                            self.ts("dve", rr.t[:, 4:8], rr.t[:, 4:8], nlam, ALU.mult, [rr, lsm], [rr])
                            for b in range(4):
                                self.stt("dve", res.t[:, b, :], psO.t[:, b, 0:128], rr.t[:, 4 + b:5 + b], tmp1.t[:, b, :],
                                         ALU.mult, ALU.add, [psO, rr, tmp1], [res])
                    for b in range(4):
                        self.act(junk.t[:, :], res.t[:, b, :], AF.Square, [res], [junk, rr], accum=rr.t[:, 8 + b:9 + b])
                    self.ts("dve", rr.t[:, 12:16], rr.t[:, 8:12], 1.0 / 128.0, ALU.mult, [rr], [rr], s2=NORM_EPS, op1=ALU.add)
                    self.act(rr.t[:, 12:16], rr.t[:, 12:16], AF.Sqrt, [rr], [rr])
                    self.recip(rr.t[:, 12:16], rr.t[:, 12:16], [rr], [rr])
                    for b in range(4):
                        self.stt("dve", hab.t[:, b, :], res.t[:, b, :], rr.t[:, 12 + b:13 + b], anw.t[:, h * 128:(h + 1) * 128],
                                 ALU.mult, ALU.mult, [res, rr, anw], [hab])
                        self.tr(psT.t[:, b, :], hab.t[:, b, :], self.cb["ident"], [hab] + CR, [psT])
                    self.act(haT[p_].t[:, g * 512:(g + 1) * 512], psT.t[:, 0:4, :], AF.Copy, [psT], [haT[p_]])
                self.dma("sp", self.mixT_d[1024 + h * 128: 1024 + (h + 1) * 128, :], haT[p_].t[:, :], [haT[p_]], [])

    def stage_mlstm(self, l):
        CR = self.CR
        with Stage(self) as outer:
            tok4 = outer.sb("mtok4", [128, NT, 16])
            decs = outer.sb("mdecs", [128, M_H, 32])
            nM = outer.sb("mnM", [4, S])
            with Stage(self) as st:
                gi = st.sb("gi", [4, S])
                gf = st.sb("gf", [4, S])
                tb_ = st.sb("gb", [4, S])
                tM = st.sb("gM", [4, S])
                tP = st.sb("gP", [4, S])
                te = st.sb("ge", [4, S])
                tw = st.sb("gw", [4, S])
                keep = st.sb("gkeep", [4, S])
                strt = st.sb("gstrt", [4, S])
                gb = st.sb("ggb", [4, 4])
                self.dma("sp", gi.t[0:4, :], self.gsc[0:4, :], [], [gi])
                self.dma("sp", gf.t[0:4, :], self.gsc[4:8, :], [], [gf])
                self.dma("sp", keep.t[0:4, :], self.crow[0, 0:S].partition_broadcast(4), [], [keep])
                self.dma("sp", strt.t[0:4, :], self.crow[0, S:2 * S].partition_broadcast(4), [], [strt])
                self.dma("sp", gb.t[0:4, 0:1], self.m_gate_b[l][0:4].rearrange("(h o) -> h o", o=1), [], [gb])
                self.dma("sp", gb.t[0:4, 1:2], self.m_gate_b[l][4:8].rearrange("(h o) -> h o", o=1), [], [gb])
                self.ts("dve", gb.t[0:4, 2:3], gb.t[0:4, 1:2], -1.0, ALU.mult, [gb], [gb])
                self.ts("dve", gi.t[0:4, :], gi.t[0:4, :], gb.t[0:4, 0:1], ALU.add, [gi, gb], [gi])
                self.act(gf.t[0:4, :], gf.t[0:4, :], AF.Exp, [gf, gb], [gf], bias=gb.t[0:4, 2:3], scale=-1.0)
                self.act(gf.t[0:4, :], gf.t[0:4, :], AF.Ln, [gf], [gf], bias=1.0)
                self.ts("dve", gf.t[0:4, :], gf.t[0:4, :], -1.0, ALU.mult, [gf], [gf])
                self.scan(tb_.t[0:4, :], keep.t[0:4, :], gf.t[0:4, :], 0.0, ALU.mult, ALU.add, [keep, gf], [tb_])
                self.tt("dve", gi.t[0:4, :], gi.t[0:4, :], tb_.t[0:4, :], ALU.subtract, [gi, tb_], [gi])
                self.ms("dve", gf.t[0:4, 0:1], 0.0, [gf])
                self.tt("dve", gf.t[0:4, 1:S], tb_.t[0:4, 0:S - 1], strt.t[0:4, 1:S], ALU.mult, [tb_, strt], [gf])
                self.scan(tM.t[0:4, :], gf.t[0:4, :], gi.t[0:4, :], 0.0, ALU.add, ALU.max, [gf, gi], [tM])
                self.tt("dve", gf.t[0:4, 1:S], tb_.t[0:4, 0:S - 1], tM.t[0:4, 0:S - 1], ALU.add, [tb_, tM], [gf])
                self.tt("dve", gf.t[0:4, 1:S], gf.t[0:4, 1:S], strt.t[0:4, 1:S], ALU.mult, [gf, strt], [gf])
                self.scan(tP.t[0:4, :], keep.t[0:4, :], gf.t[0:4, :], 0.0, ALU.mult, ALU.add, [keep, gf], [tP])
                self.tt("dve", tP.t[0:4, :], tP.t[0:4, :], tM.t[0:4, :], ALU.subtract, [tP, tM], [tP])
                self.act(tP.t[0:4, :], tP.t[0:4, :], AF.Exp, [tP], [tP])
                self.tt("dve", te.t[0:4, :], tb_.t[0:4, :], tM.t[0:4, :], ALU.add, [tb_, tM], [te])
                self.act(te.t[0:4, :], te.t[0:4, :], AF.Exp, [te], [te], scale=-1.0)
                self.ts("dve", gi.t[0:4, :], gi.t[0:4, :], LN16, ALU.add, [gi], [gi])
                M3 = tM.t[0:4, :].rearrange("p (c t) -> p c t", t=64)
                self.tt("dve", tw.t[0:4, :].rearrange("p (c t) -> p c t", t=64), gi.t[0:4, :].rearrange("p (c t) -> p c t", t=64),
                        M3[:, :, 63:64].to_broadcast([4, 32, 64]), ALU.subtract, [gi, tM], [tw])
                self.act(tw.t[0:4, :], tw.t[0:4, :], AF.Exp, [tw], [tw])
                self.ts("dve", nM.t[0:4, :], tM.t[0:4, :], -1.0, ALU.mult, [tM], [nM])
                pq = st.ps("gpq", [128, NT, 32])
                for tb in range(NT):
                    for qi, src in enumerate((gi, tw, tP, te)):
                        self.mm(pq.t[:, tb, qi * 4:(qi + 1) * 4], src.t[0:4, tb * 128:(tb + 1) * 128], self.c["ident"][0:4, 0:4],
                                True, True, [src] + CR, [pq])
                self.cp("dve", tok4.t[:, :, :], pq.t[:, :, 0:16], [pq], [tok4])
                pd = st.ps("gpd", [128, M_H, 128])
                wl = tP.t[0:4, :].rearrange("p (c t) -> p c t", t=64)[:, :, 63]
                for hd in range(M_H):
                    self.mm(pd.t[:, hd, 0:32], self.c["sel%d" % hd][0:4, :], wl, True, True, [tP] + CR, [pd])
                self.cp("dve", decs.t[:, :, :], pd.t[:, :, 0:32], [pd], [decs])
            with Stage(self) as st:
                hT = self.load_hT(st)
                mnw = self.load_bc(st, "mnw", self.m_norm_w[l], M_W)
                cwT = st.sb("cwT", [8, 2 * M_W])
                self.dma("sp", cwT.t[0:4, :], self.m_conv_w[l], [], [cwT])
                self.dma("sp", cwT.t[4:5, :], self.m_conv_b[l:l + 1, :], [], [cwT])
                cw = st.sb("cw", [128, 16, 8])
                with Stage(self) as s2:
                    pc = s2.ps("mpc", [128, 16, 32])
                    for ch in range(16):
                        self.tr(pc.t[:, ch, 0:5], cwT.t[0:5, ch * 128:(ch + 1) * 128], self.c["ident"][0:5, 0:5], [cwT] + CR, [pc])
                    self.cp("dve", cw.t[:, :, 0:5], pc.t[:, :, 0:5], [pc], [cw])
                wqk = [st.sb("mwqk", [128, KC, 128], BF16) for _ in range(2)]
                wvo = st.sb("mwvo", [128, KC, 512], BF16)
                xpad = st.sb("mxpad", [128, S + 4])
                self.ms("dve", xpad.t[:, 0:3], 0.0, [xpad])
                cacc = st.sb("mcacc", [128, S])
                qz = [st.sb("mqz0", [128, 2, S], BF16), st.sb("mqz1", [128, 2, S], BF16)]
                qk = [qz[0], st.sb("mkT", [128, 2, S], BF16)]
                vaug = st.sb("mvaug", [128, NT, 260], BF16)
                self.ms("dve", vaug.t[:, :, 256:260], 1.0, [vaug])
                og = st.sb("mog", [128, NT, 256], BF16)
                hmT = st.sb("mhmT", [128, 2, S], BF16)
                Cst = st.sb("mC", [128, 2, 260])
                Cb = [st.sb("mCb", [128, 2, 260], BF16) for _ in range(2)]
                DT = [st.sb("mDT", [128, 128]) for _ in range(2)]
                ST = [st.sb("mST", [128, 128], BF16) for _ in range(2)]
                kws = [st.sb("mkws", [128, 2, 128], BF16) for _ in range(2)]
                Bs = [st.sb("mBs", [128, 260]) for _ in range(2)]
                nd = [st.sb("mnd", [128, 260]) for _ in range(2)]
                sm = [st.sb("msm", [128, 8]) for _ in range(2)]
                junk = st.sb("mjunk", [128, 256])
                tmph = [st.sb("mtmph", [128, 256]) for _ in range(2)]
                hmb = [st.sb("mhmb", [128, 256], BF16) for _ in range(2)]
                psA = [st.ps("mpsA", [128, 512]) for _ in range(2)]
                psSB = st.ps("mpsSB", [128, 512])
                psN = st.ps("mpsN", [128, 512])
                psI = st.ps("mpsI", [128, 512])
                psK = st.ps("mpsK", [128, 1024], BF16)
                psV = st.ps("mpsV", [128, 2, 512])
                cA = 0
                cW = 0
                for hd in range(M_H):
                    for j, off in enumerate((O_MV, O_MO)):
                        src = self.w_in[l][:, off + hd * 256: off + (hd + 1) * 256].rearrange("(kc p) c -> p kc c", p=128)
                        self.dma("pool", wvo.t[:, :, j * 256:(j + 1) * 256], src, [], [wvo.sub(j)])
                    for which, off in enumerate((O_MQ, O_MK)):
                        for dc in range(2):
                            w_ = wqk[cW % 2]
                            cW += 1
                            src = self.w_in[l][:, off + hd * 256 + dc * 128: off + hd * 256 + (dc + 1) * 128].rearrange("(kc p) c -> p kc c", p=128)
                            self.dma("pool", w_.t[:, :, :], src, [], [w_])
                            ch = which * 8 + hd * 2 + dc
                            for tg in range(4):
                                pa = psA[cA % 2]
                                cA += 1
                                for kc in range(KC):
                                    self.mm(pa.t[:, :], w_.t[:, kc, :], hT.t[:, kc, tg * 512:(tg + 1) * 512], kc == 0, kc == KC - 1,
                                            [w_, hT.sub(kc)], [pa])
                                self.act(xpad.t[:, 3 + tg * 512: 3 + (tg + 1) * 512], pa.t[:, :], AF.Copy, [pa], [xpad])
                            self.ts("dve", cacc.t[:, :], xpad.t[:, 3:3 + S], cw.t[:, ch, 3:4], ALU.mult, [xpad, cw], [cacc],
                                    s2=cw.t[:, ch, 4:5], op1=ALU.add)
                            for j in range(3):
                                self.stt("dve", cacc.t[:, :], xpad.t[:, j:j + S], cw.t[:, ch, j:j + 1], cacc.t[:, :], ALU.mult, ALU.add,
                                         [xpad, cw, cacc], [cacc])
                            self.act(qk[which].t[:, dc, :], cacc.t[:, :], AF.Silu, [cacc], [qk[which]])
                    self.cp("pool", qz[1].t[:, :, :], qz[0].t[:, :, :], [qz[0]], [qz[1]])
                    for z in range(2):
                        zv = qz[z].t[:, :, :].rearrange("p d (b c t) -> p d b c t", c=2, t=64)[:, :, :, 1 - z, :]
                        self.ms("pool", zv, 0.0, [qz[z]])
                    for tb in range(NT):
                        pa = psA[cA % 2]
                        cA += 1
                        for kc in range(KC):
                            self.mm(pa.t[:, :], hT.t[:, kc, tb * 128:(tb + 1) * 128], wvo.t[:, kc, :], kc == 0, kc == KC - 1,
                                    [wvo.sub(0), wvo.sub(1), hT.sub(kc)], [pa])
                        self.act(vaug.t[:, tb, 0:256], pa.t[:, 0:256], AF.Copy, [pa], [vaug.sub(tb)])
                        self.act(og.t[:, tb, :], pa.t[:, 256:512], AF.Sigmoid, [pa], [og.sub(tb)])
                    self.ms("dve", Cst.t[:, :, :], 0.0, [Cst])
                    self.ms("dve", Cb[0].t[:, :, :], 0.0, [Cb[0]])
                    cbi = 0
                    for tb in range(NT):
                        i2 = tb % 2
                        tsl = slice(tb * 128, (tb + 1) * 128)
                        for z in range(2):
                            for dc in range(2):
                                self.mm(psSB.t[:, 0:128], qk[1].t[:, dc, tsl], qz[z].t[:, dc, tsl], z == 0 and dc == 0, z == 1 and dc == 1,
                                        [qz[z], qk[1]], [psSB])
                        self.mm(psSB.t[:, 128:256], self.c["sel%d" % hd][0:4, :], nM.t[0:4, tsl], True, False, [nM] + CR, [psSB])
                        self.mm(psSB.t[:, 128:256], self.c["ident"], self.c["neg128"], False, True, CR, [psSB])
                        self.act(DT[i2].t[:, :], psSB.t[:, 128:256], AF.Exp, [psSB, tok4], [DT[i2]], bias=tok4.t[:, tb, hd:hd + 1])
                        self.tt("dve", ST[i2].t[:, :], psSB.t[:, 0:128], DT[i2].t[:, :], ALU.mult, [psSB, DT[i2]], [ST[i2]])
                        self.mm(psN.t[:, 0:257], ST[i2].t[:, :], vaug.t[:, tb, 0:257], True, True, [ST[i2], vaug.sub(tb)], [psN])
                        for dc in range(2):
                            self.tr(psK.t[:, dc * 128:(dc + 1) * 128], qk[1].t[:, dc, tsl], self.cb["ident"], [qk[1]] + CR, [psK])
                        self.act(kws[i2].t[:, :, :], psK.t[:, 0:256].rearrange("p (d e) -> p d e", e=128), AF.Copy, [psK, tok4], [kws[i2]],
                                 scale=tok4.t[:, tb, 4 + hd:5 + hd])
                        for c2 in range(2):
                            cb_cur = Cb[cbi % 2]
                            for dc in range(2):
                                self.mm(psI.t[:, 0:257], qz[c2].t[:, dc, tsl], cb_cur.t[:, dc, 0:257], c2 == 0 and dc == 0, c2 == 1 and dc == 1,
                                        [qz[c2], cb_cur], [psI])
                            prt = slice(c2 * 64, (c2 + 1) * 64)
                            for dc in range(2):
                                self.mm(psV.t[:, dc, 0:257], kws[i2].t[prt, dc, :], vaug.t[prt, tb, 0:257], True, True,
                                        [kws[i2], vaug.sub(tb)], [psV.sub(dc)])
                            cb_nxt = Cb[(cbi + 1) % 2]
                            for dc in range(2):
                                self.stt("dve", Cst.t[:, dc, 0:257], Cst.t[:, dc, 0:257], decs.t[:, hd, tb * 2 + c2: tb * 2 + c2 + 1],
                                         psV.t[:, dc, 0:257], ALU.mult, ALU.add, [Cst, decs, psV.sub(dc)], [Cst])
                            self.act(cb_nxt.t[:, :, 0:257], Cst.t[:, :, 0:257], AF.Copy, [Cst], [cb_nxt])
                            cbi += 1
                        self.act(Bs[i2].t[:, 0:257], psI.t[:, 0:257], AF.Copy, [psI, tok4], [Bs[i2]], scale=tok4.t[:, tb, 8 + hd:9 + hd])
                        self.tt("dve", nd[i2].t[:, 0:257], psN.t[:, 0:257], Bs[i2].t[:, 0:257], ALU.add, [psN, Bs[i2]], [nd[i2]])
                        s_ = sm[i2]
                        self.act(s_.t[:, 6:7], nd[i2].t[:, 256:257], AF.Abs, [nd[i2]], [s_])
                        self.ts("dve", s_.t[:, 0:1], s_.t[:, 6:7], tok4.t[:, tb, 12 + hd:13 + hd], ALU.max, [s_, tok4], [s_])
                        self.recip(s_.t[:, 1:2], s_.t[:, 0:1], [s_], [s_])
                        self.act(junk.t[:, :], nd[i2].t[:, 0:256], AF.Square, [nd[i2], s_], [junk, s_], scale=s_.t[:, 1:2], accum=s_.t[:, 2:3])
                        self.ts("dve", s_.t[:, 3:4], s_.t[:, 2:3], 1.0 / 256.0, ALU.mult, [s_], [s_], s2=NORM_EPS, op1=ALU.add)
                        self.act(s_.t[:, 3:4], s_.t[:, 3:4], AF.Sqrt, [s_], [s_])
                        self.recip(s_.t[:, 4:5], s_.t[:, 3:4], [s_], [s_])
                        self.tt("dve", s_.t[:, 5:6], s_.t[:, 4:5], s_.t[:, 1:2], ALU.mult, [s_], [s_])
                        self.stt("dve", tmph[i2].t[:, :], nd[i2].t[:, 0:256], s_.t[:, 5:6], mnw.t[:, hd * 256:(hd + 1) * 256], ALU.mult, ALU.mult,
                                 [nd[i2], s_, mnw], [tmph[i2]])
                        self.tt("pool", hmb[i2].t[:, :], tmph[i2].t[:, :], og.t[:, tb, :], ALU.mult, [tmph[i2], og.sub(tb)], [hmb[i2]])
                        for dc in range(2):
                            self.tr(psK.t[:, 256 + dc * 128: 256 + (dc + 1) * 128], hmb[i2].t[:, dc * 128:(dc + 1) * 128], self.cb["ident"],
                                    [hmb[i2]] + CR, [psK])
                        self.cp("dve", hmT.t[:, :, tsl], psK.t[:, 256:512].rearrange("p (d e) -> p d e", e=128), [psK], [hmT])
                    for dc in range(2):
                        self.dma("sp", self.mixT_d[hd * 256 + dc * 128: hd * 256 + (dc + 1) * 128, :], hmT.t[:, dc, :], [hmT], [])

    def stage_hgrn(self, l):
        CR = self.CR
        with Stage(self) as st:
            hT = self.load_hT(st)
            gnw = self.load_bc(st, "gnw", self.g_norm_w[l], G_W)
            lb = st.sb("glb", [128, G_W])
            oml = st.sb("goml", [128, G_W])
            with Stage(self) as s2:
                lg = [self.load_bc(s2, "glg", self.g_lb[j], G_W) for j in range(DEPTH)]
                mxl = s2.sb("gmxl", [128, G_W])
                sme = s2.sb("gsme", [128, G_W])
                self.tt("dve", mxl.t[:, :], lg[0].t[:, :], lg[1].t[:, :], ALU.max, [lg[0], lg[1]], [mxl])
                for j in range(DEPTH):
                    self.tt("dve", lg[j].t[:, :], lg[j].t[:, :], mxl.t[:, :], ALU.subtract, [lg[j], mxl], [lg[j]])
                    self.act(lg[j].t[:, :], lg[j].t[:, :], AF.Exp, [lg[j]], [lg[j]])
                self.tt("dve", sme.t[:, :], lg[0].t[:, :], lg[1].t[:, :], ALU.add, [lg[0], lg[1]], [sme])
                self.recip(sme.t[:, :], sme.t[:, :], [sme], [sme])
                for j in range(DEPTH):
                    self.tt("dve", lg[j].t[:, :], lg[j].t[:, :], sme.t[:, :], ALU.mult, [lg[j], sme], [lg[j]])
                self.cp("dve", mxl.t[:, :], lg[0].t[:, :], [lg[0]], [mxl])
                for j in range(1, l + 1):
                    self.tt("dve", mxl.t[:, :], mxl.t[:, :], lg[j].t[:, :], ALU.add, [mxl, lg[j]], [mxl])
                self.tt("dve", lb.t[:, :], mxl.t[:, :], lg[0].t[:, :], ALU.subtract, [mxl, lg[0]], [lb])
                self.ts("dve", oml.t[:, :], lb.t[:, :], -1.0, ALU.mult, [lb], [oml], s2=1.0, op1=ALU.add)
            wg = [st.sb("gwg", [128, KC, 512], BF16) for _ in range(2)]
            f32t = lambda nm: [st.sb(nm, [128, 128]) for _ in range(2)]
            bft = lambda nm: [st.sb(nm, [128, 128], BF16) for _ in range(2)]
            tq, tf, tlf, tk, tgg, teq, tek = (f32t("gq"), f32t("gf"), f32t("glf"), f32t("gk"), f32t("ggg"), f32t("geq"), f32t("gek"))
            tqt, tkt, tvb, tam, tktT, thgb = (bft("gqt"), bft("gkt"), bft("gvb"), bft("gam"), bft("gktT"), bft("ghgb"))
            qz0 = bft("gqz0")
            qz1 = bft("gqz1")
            for i in range(2):
                self.ms("dve", qz0[i].t[:, :], 0.0, [qz0[i]])
                self.ms("dve", qz1[i].t[:, :], 0.0, [qz1[i]])
            tE = [st.sb("gE", [128, 8]) for _ in range(2)]
            tsm = [st.sb("gsm", [128, 8]) for _ in range(2)]
            ttmp = f32t("gtmp")
            junk = st.sb("gjunk", [128, 128])
            Sst = st.sb("gS", [128, 128])
            Sdec = st.sb("gSdec", [128, 128])
            Sb = [st.sb("gSb", [128, 128], BF16) for _ in range(2)]
            hgT = [st.sb("ghgT", [128, S], BF16) for _ in range(2)]
            psA = [st.ps("gpsA", [128, 512]) for _ in range(2)]
            psB = st.ps("gpsB", [128, 512])
            psE = st.ps("gpsE", [128, 512])
            psT = st.ps("gpsT", [128, 8, 128], BF16)
            psAt = st.ps("gpsAt", [128, 512])
            psO = st.ps("gpsO", [128, 512])
            psKV = st.ps("gpsKV", [128, 512])
            cA = 0
            import os as _os
            _nh = int(_os.environ.get("HG_HEADS", G_H)); _nb = int(_os.environ.get("HG_BLOCKS", NT)); _lvl = int(_os.environ.get("HG_LVL", 9))
            for h in range(_nh):
                w_ = wg[h % 2]
                for j, off in enumerate((O_GQ, O_GF, O_GI, O_GG)):
                    src = self.w_in[l][:, off + h * 128: off + (h + 1) * 128].rearrange("(kc p) c -> p kc c", p=128)
                    self.dma("pool", w_.t[:, :, j * 128:(j + 1) * 128], src, [], [w_.sub(j)])
                wdeps = [w_.sub(j) for j in range(4)]
                hs = slice(h * 128, (h + 1) * 128)
                self.ms("dve", Sst.t[:, :], 0.0, [Sst])
                for tb in range(_nb):
                    i = tb % 2
                    pa = psA[cA % 2]
                    cA += 1
                    for kc in range(KC):
                        self.mm(pa.t[:, :], hT.t[:, kc, tb * 128:(tb + 1) * 128], w_.t[:, kc, :], kc == 0, kc == KC - 1,
                                wdeps + [hT.sub(kc)], [pa])
                    if _lvl < 1:
                        continue
                    self.act(tq[i].t[:, :], pa.t[:, 0:128], AF.Silu, [pa], [tq[i]])
                    self.act(tf[i].t[:, :], pa.t[:, 128:256], AF.Sigmoid, [pa], [tf[i]])
                    self.act(tvb[i].t[:, :], pa.t[:, 256:384], AF.Copy, [pa], [tvb[i]])
                    self.act(tgg[i].t[:, :], pa.t[:, 384:512], AF.Silu, [pa], [tgg[i]])
                    self.tt("dve", tf[i].t[:, :], tf[i].t[:, :], oml.t[:, hs], ALU.mult, [tf[i], oml], [tf[i]])
                    self.tt("dve", tf[i].t[:, :], tf[i].t[:, :], lb.t[:, hs], ALU.add, [tf[i], lb], [tf[i]])
                    self.ts("dve", tf[i].t[:, :], tf[i].t[:, :], 1e-30, ALU.max, [tf[i]], [tf[i]])
                    self.act(tlf[i].t[:, :], tf[i].t[:, :], AF.Ln, [tf[i]], [tlf[i]])
                    self.ts("dve", tk[i].t[:, :], tf[i].t[:, :], -1.0, ALU.mult, [tf[i]], [tk[i]], s2=1.0, op1=ALU.add)
                    if _lvl < 2:
                        continue
                    self.mm(psB.t[:, 0:128], self.c["M1"], tlf[i].t[:, :], True, True, [tlf[i]] + CR, [psB])
                    _sub = int(_os.environ.get("HG_SUB", 9))
                    if _sub < 1:
                        continue
                    self.mm(psE.t[:, 0:8], tlf[i].t[:, :], self.c["Mref"][:, 0:8], True, True, [tlf[i]] + CR, [psE])
                    if _sub < 2:
                        if tb == 0 and h == 0:
                            self.dump("tf", tf[i].t[:, :], [128, 128], [tf[i]])
                            self.dump("tlf", tlf[i].t[:, :], [128, 128], [tlf[i]])
                            self.cp("dve", junk.t[:, :], psB.t[:, 0:128], [psB], [junk])
                            self.dump("pb", junk.t[:, :], [128, 128], [junk])
                            self.dump("oml", oml.t[:, :], [128, 1024], [oml])
                        continue
                    self.act(teq[i].t[:, :], psB.t[:, 0:128], AF.Exp, [psB], [teq[i]])
                    self.act(tek[i].t[:, :], psB.t[:, 0:128], AF.Exp, [psB], [tek[i]], scale=-1.0)
                    if _sub < 3:
                        continue
                    self.act(tE[i].t[:, 0:8], psE.t[:, 0:8], AF.Exp, [psE], [tE[i]])
                    if _sub < 4:
                        continue
                    self.tt("dve", tqt[i].t[:, :], tq[i].t[:, :], teq[i].t[:, :], ALU.mult, [tq[i], teq[i]], [tqt[i]])
                    self.tt("dve", tkt[i].t[:, :], tk[i].t[:, :], tek[i].t[:, :], ALU.mult, [tk[i], tek[i]], [tkt[i]])
                    if _lvl < 3:
                        continue
                    self.tr(psT.t[:, 0, :], tqt[i].t[:, :], self.cb["ident"], [tqt[i]] + CR, [psT])
                    self.tr(psT.t[:, 1, :], tkt[i].t[:, :], self.cb["ident"], [tkt[i]] + CR, [psT])
                    _s3 = int(_os.environ.get("HG_S3", 9))
                    if _s3 < 1:
                        continue
                    self.cp("dve", qz0[i].t[:, 0:64], psT.t[:, 0, 0:64], [psT], [qz0[i]])
                    if _s3 < 2:
                        continue
                    self.cp("dve", qz1[i].t[:, 64:128], psT.t[:, 0, 64:128], [psT], [qz1[i]])
                    if _s3 < 3:
                        continue
                    self.act(tktT[i].t[:, :], psT.t[:, 1, :], AF.Copy, [psT], [tktT[i]])
                    if _lvl < 4:
                        continue
                    self.mm(psAt.t[:, 0:128], tktT[i].t[:, :], qz0[i].t[:, :], True, False, [tktT[i], qz0[i]], [psAt])
                    self.mm(psAt.t[:, 0:128], tktT[i].t[:, :], qz1[i].t[:, :], False, True, [tktT[i], qz1[i]], [psAt])
                    self.tt("dve", tam[i].t[:, :], psAt.t[:, 0:128], self.c["maskG"], ALU.mult, [psAt] + CR, [tam[i]])
                    self.mm(psO.t[:, 0:128], tam[i].t[:, :], tvb[i].t[:, :], True, False, [tam[i], tvb[i]], [psO])
                    for c2 in range(2):
                        qz = qz0[i] if c2 == 0 else qz1[i]
                        sb_ = Sb[c2]
                        self.ts("dve", sb_.t[:, :], Sst.t[:, :], tE[i].t[:, 2 * c2: 2 * c2 + 1], ALU.mult, [Sst, tE[i]], [sb_])
                        self.mm(psO.t[:, 0:128], qz.t[:, :], sb_.t[:, :], False, c2 == 1, [qz, sb_], [psO])
                        self.ts("dve", Sdec.t[:, :], Sst.t[:, :], tE[i].t[:, 4 + c2: 5 + c2], ALU.mult, [Sst, tE[i]], [Sdec])
                        prt = slice(c2 * 64, (c2 + 1) * 64)
                        self.mm(psKV.t[:, c2 * 128:(c2 + 1) * 128], tkt[i].t[prt, :], tvb[i].t[prt, :], True, True,
                                [tkt[i], tvb[i]], [psKV])
                        self.stt("dve", Sst.t[:, :], psKV.t[:, c2 * 128:(c2 + 1) * 128], tE[i].t[:, 2 * c2 + 1: 2 * c2 + 2], Sdec.t[:, :],
                                 ALU.mult, ALU.add, [psKV, tE[i], Sdec], [Sst])
                    if _lvl < 5:
                        continue
                    s_ = tsm[i]
                    self.act(junk.t[:, :], psO.t[:, 0:128], AF.Square, [psO], [junk, s_], accum=s_.t[:, 0:1])
                    self.ts("dve", s_.t[:, 1:2], s_.t[:, 0:1], 1.0 / 128.0, ALU.mult, [s_], [s_], s2=NORM_EPS, op1=ALU.add)
                    self.act(s_.t[:, 1:2], s_.t[:, 1:2], AF.Sqrt, [s_], [s_])
                    self.recip(s_.t[:, 2:3], s_.t[:, 1:2], [s_], [s_])
                    self.stt("dve", ttmp[i].t[:, :], psO.t[:, 0:128], s_.t[:, 2:3], gnw.t[:, hs], ALU.mult, ALU.mult, [psO, s_, gnw], [ttmp[i]])
                    self.tt("pool", thgb[i].t[:, :], ttmp[i].t[:, :], tgg[i].t[:, :], ALU.mult, [ttmp[i], tgg[i]], [thgb[i]])
                    self.tr(psT.t[:, 2, :], thgb[i].t[:, :], self.cb["ident"], [thgb[i]] + CR, [psT])
                    self.act(hgT[h % 2].t[:, tb * 128:(tb + 1) * 128], psT.t[:, 2, :], AF.Copy, [psT], [hgT[h % 2]])
                self.dma("sp", self.mixT_d[2048 + h * 128: 2048 + (h + 1) * 128, :], hgT[h % 2].t[:, :], [hgT[h % 2]], [])

    def stage_proj(self, l):
        pw = (self.p_m, self.p_a, self.p_g)
        with Stage(self) as st:
            hT = self.load_hT(st)
            mix = [st.sb("pmix", [128, 8, 1024], BF16) for _ in range(3)]
            wps = [st.sb("pwp", [128, 3, 8, 128], BF16) for _ in range(2)]
            wgs = [st.sb("pwg", [128, 3, KC, 128], BF16) for _ in range(2)]
            ych = [st.sb("pych", [128, 1024], BF16) for _ in range(2)]
            sgt = [st.sb("psgt", [128, 512]) for _ in range(2)]
            yacc = [st.sb("pyacc", [128, 512]) for _ in range(2)]
            tmp = [st.sb("ptmp", [128, 512]) for _ in range(2)]
            psG = [st.ps("ppsG", [128, 512]) for _ in range(2)]
            psP = [st.ps("ppsP", [128, 512]) for _ in range(2)]
            kq = 0
            cc = 0
            for th in range(2):
                for i in range(3):
                    for k8 in range(8):
                        self.dma("sp", mix[i].t[:, k8, :], self.mixT_d[i * 1024 + k8 * 128: i * 1024 + (k8 + 1) * 128, th * 1024:(th + 1) * 1024],
                                 [], [mix[i]])
                for c in range(16):
                    wp, wgt, yc = wps[cc % 2], wgs[cc % 2], ych[cc % 2]
                    cc += 1
                    for i in range(3):
                        self.dma("pool", wp.t[:, i, :, :], pw[i][l][:, c * 128:(c + 1) * 128].rearrange("(kc p) c -> p kc c", p=128), [], [wp.sub(i)])
                        o0 = O_GATE + i * D + c * 128
                        self.dma("pool", wgt.t[:, i, :, :], self.w_in[l][:, o0:o0 + 128].rearrange("(kc p) c -> p kc c", p=128), [], [wgt.sub(i)])
                    for tg in range(2):
                        ya = yacc[tg]
                        for i in range(3):
                            pg, pp, sg, tm = psG[kq % 2], psP[kq % 2], sgt[kq % 2], tmp[kq % 2]
                            kq += 1
                            t0 = th * 1024 + tg * 512
                            for kc in range(KC):
                                self.mm(pg.t[:, :], wgt.t[:, i, kc, :], hT.t[:, kc, t0:t0 + 512], kc == 0, kc == KC - 1, [wgt.sub(i), hT.sub(kc)], [pg])
                            for k8 in range(8):
                                self.mm(pp.t[:, :], wp.t[:, i, k8, :], mix[i].t[:, k8, tg * 512:(tg + 1) * 512], k8 == 0, k8 == 7, [wp.sub(i), mix[i]], [pp])
                            self.act(sg.t[:, :], pg.t[:, :], AF.Sigmoid, [pg], [sg])
                            if i == 0:
                                self.tt("dve", ya.t[:, :], pp.t[:, :], sg.t[:, :], ALU.mult, [pp, sg], [ya])
                            else:
                                self.tt("dve", tm.t[:, :], pp.t[:, :], sg.t[:, :], ALU.mult, [pp, sg], [tm])
                                if i == 1:
                                    self.tt("pool", ya.t[:, :], ya.t[:, :], tm.t[:, :], ALU.add, [ya, tm], [ya])
                                else:
                                    self.tt("pool", yc.t[:, tg * 512:(tg + 1) * 512], ya.t[:, :], tm.t[:, :], ALU.add, [ya, tm], [yc])
                    self.dma("sp", self.yT_d[c * 128:(c + 1) * 128, th * 1024:(th + 1) * 1024], yc.t[:, :], [yc], [])

    def stage_wout(self, l):
        with Stage(self) as st:
            wo = st.sb("wo", [128, KC, D], BF16)
            for kc in range(KC):
                self.dma("pool", wo.t[:, kc, :], self.w_out[l][kc * 128:(kc + 1) * 128, :], [], [wo.sub(kc)])
            lnw = self.load_bc(st, "lnw", self.ln1_w[l], D)
            lnb = self.load_bc(st, "lnb", self.ln1_b[l], D)
            a = self.ln_alloc(st)
            yT = st.sb("wyT", [128, KC, 512], BF16)
            hin = [st.sb("whin", [128, D]) for _ in range(2)]
            z = st.sb("wz", [128, D])
            psM = [st.ps("wpsM", [128, 512]) for _ in range(4)]
            yv = self.yT_d.rearrange("(kc p) t -> p kc t", p=128)
            for tb in range(NT):
                if tb % 4 == 0:
                    g = tb // 4
                    self.dma("sp", yT.t[:, :, :], yv[:, :, g * 512:(g + 1) * 512], [], [yT])
                hi = hin[tb % 2]
                self.dma("sp", hi.t[:, :], self.hres[tb * 128:(tb + 1) * 128, :], [], [hi])
                for cg in range(4):
                    pm = psM[cg]
                    for kc in range(KC):
                        self.mm(pm.t[:, :], yT.t[:, kc, (tb % 4) * 128:(tb % 4 + 1) * 128], wo.t[:, kc, cg * 512:(cg + 1) * 512],
                                kc == 0, kc == KC - 1, [yT, wo.sub(kc)], [pm])
                    self.stt("dve", z.t[:, cg * 512:(cg + 1) * 512], hi.t[:, cg * 512:(cg + 1) * 512], ALPHA, pm.t[:, :], ALU.mult, ALU.add,
                             [hi, pm], [z])
                self.ln_block(a, z.t[:, :], [z], lnw, lnb, tb, self.hres)

    def stage_ffn(self, l):
        last = (l == DEPTH - 1)
        with Stage(self) as st:
            acc = st.sb("facc", [128, 8, D])
            for th in range(2):
                with Stage(self) as s1:
                    h1T = s1.sb("fh1T", [128, KC, 1024], BF16)
                    for kc in range(KC):
                        self.dma("sp", h1T.t[:, kc, :], self.hT_d[kc * 128:(kc + 1) * 128, th * 1024:(th + 1) * 1024], [], [h1T.sub(kc)])
                    wus = [s1.sb("fwu", [128, KC, 512], BF16) for _ in range(2)]
                    wds = [s1.sb("fwd", [128, 4, D], BF16) for _ in range(2)]
                    hids = [s1.sb("fhid", [128, 4, 1024], BF16) for _ in range(2)]
                    rts = [s1.sb("frt", [128, 512]) for _ in range(2)]
                    psU = [s1.ps("fpsU", [128, 512]) for _ in range(2)]
                    psD = [s1.ps("fpsD", [128, 512]) for _ in range(4)]
                    ku = 0
                    kd = 0
                    for sc in range(16):
                        wu, wd, hid = wus[sc % 2], wds[sc % 2], hids[sc % 2]
                        self.dma("pool", wu.t[:, :, :], self.w_up[l][:, sc * 512:(sc + 1) * 512].rearrange("(kc p) c -> p kc c", p=128), [], [wu])
                        self.dma("pool", wd.t[:, :, :], self.w_down[l][sc * 512:(sc + 1) * 512, :].rearrange("(fc p) c -> p fc c", p=128), [], [wd])
                        for fc in range(4):
                            for tg in range(2):
                                pu, rt = psU[ku % 2], rts[ku % 2]
                                ku += 1
                                for kc in range(KC):
                                    self.mm(pu.t[:, :], wu.t[:, kc, fc * 128:(fc + 1) * 128], h1T.t[:, kc, tg * 512:(tg + 1) * 512],
                                            kc == 0, kc == KC - 1, [wu, h1T.sub(kc)], [pu])
                                self.act(rt.t[:, :], pu.t[:, :], AF.Relu, [pu], [rt])
                                self.tt("pool", hid.t[:, fc, tg * 512:(tg + 1) * 512], rt.t[:, :], rt.t[:, :], ALU.mult, [rt], [hid.sub((fc, tg))])
                        hdeps = [hid.sub((fc, tg)) for fc in range(4) for tg in range(2)]
                        for tb in range(8):
                            for cg in range(4):
                                pd = psD[kd % 4]
                                kd += 1
                                for fc in range(4):
                                    self.mm(pd.t[:, :], hid.t[:, fc, tb * 128:(tb + 1) * 128], wd.t[:, fc, cg * 512:(cg + 1) * 512],
                                            fc == 0, fc == 3, hdeps + [wd], [pd])
                                dst = acc.t[:, tb, cg * 512:(cg + 1) * 512]
                                if sc == 0:
                                    self.act(dst, pd.t[:, :], AF.Copy, [pd], [acc.sub((tb, cg))])
                                else:
                                    self.tt("dve", dst, dst, pd.t[:, :], ALU.add, [pd, acc.sub((tb, cg))], [acc.sub((tb, cg))])
                with Stage(self) as s2:
                    lnw = self.load_bc(s2, "lnw", self.ln2_w[l], D)
                    lnb = self.load_bc(s2, "lnb", self.ln2_b[l], D)
                    a = self.ln_alloc(s2)
                    hin = [s2.sb("fhin", [128, D]) for _ in range(2)]
                    z = s2.sb("fz", [128, D])
                    for tb in range(8):
                        T = th * 8 + tb
                        hi = hin[tb % 2]
                        self.dma("sp", hi.t[:, :], self.hres[T * 128:(T + 1) * 128, :], [], [hi])
                        adeps = [acc.sub((tb, cg)) for cg in range(4)]
                        self.stt("dve", z.t[:, :], hi.t[:, :], ALPHA, acc.t[:, tb, :], ALU.mult, ALU.add, [hi] + adeps, [z])
                        if last:
                            self.ln_block(a, z.t[:, :], [z], lnw, lnb, T, self.out, hT_out=False)
                        else:
                            self.ln_block(a, z.t[:, :], [z], lnw, lnb, T, self.hres)


def build_program(stages=None, dbg=False):
    nc = bass.Bass("TRN2", target_bir_lowering=False)
    k = K(nc, dbg)
    with contextlib.ExitStack() as es:
        sems = {e: es.enter_context(nc.semaphore("s_" + e)) for e in Rec.ENGS}
        dsems = [es.enter_context(nc.semaphore("d%d" % i)) for i in range(N_DMA_SEM)]
        k.setup_consts(es)
        k.R.barrier()
        if stages is None:
            stages = ["ln0"]
            for l in range(DEPTH):
                stages += ["attn%d" % l, "mlstm%d" % l, "hgrn%d" % l, "proj%d" % l, "wout%d" % l, "ffn%d" % l]
        for sname in stages:
            if sname == "ln0":
                k.stage_ln0()
            else:
                getattr(k, "stage_" + sname[:-1])(int(sname[-1]))
        k.R.barrier()
        n_ins, n_wait = k.R.emit(nc, sems, dsems)
        print("[kernel] instructions=%d waits=%d" % (n_ins, n_wait), flush=True)
    return nc, k


_PROG = None


def _get_program():
    global _PROG
    if _PROG is None:
        _PROG = build_program(None, dbg=False)
    return _PROG


def kernel(**inputs):
    nc, k = _get_program()
    _, pack, crow = host_consts()
    n_cores = 8
    shared = {}
    for n in k.declared:
        if n in ("x", "positions", "consts", "crow"):
            continue
        a = np.asarray(inputs[n])
        if n == "a_lambda":
            a = a.reshape(DEPTH, 4 * A_DH)
        shared[n] = np.ascontiguousarray(a, dtype=np.float32)
    x = np.asarray(inputs["x"], dtype=np.float32)
    pos = np.asarray(inputs["positions"]).astype(np.int32)
    in_maps = []
    for c in range(n_cores):
        b = c % x.shape[0]
        m = dict(shared)
        m["x"] = np.ascontiguousarray(x[b])
        m["positions"] = np.ascontiguousarray(pos[b:b + 1])
        m["consts"] = pack
        m["crow"] = crow
        in_maps.append({n: m[n] for n in k.declared})
    res = run_bass_kernel_spmd(nc, in_maps, core_ids=list(range(n_cores)))
    out = np.stack([np.asarray(res.results[b]["out"], dtype=np.float32) for b in range(x.shape[0])], axis=0)
    return out
```

```python
import math
import contextlib
import numpy as np
import concourse.bass as bass
import concourse.mybir as mybir
from concourse.bass_utils import run_bass_kernel_spmd

F32 = mybir.dt.float32
BF16 = mybir.dt.bfloat16
I32 = mybir.dt.int32
AF = mybir.ActivationFunctionType
ALU = mybir.AluOpType

D = 2048
S = 2048
NT = 16
KC = 16
DEPTH = 2
M_H, M_DH, M_W = 4, 256, 1024
A_H, A_DH, A_W = 8, 64, 1024
G_H, G_DK, G_W = 8, 128, 1024
D_FF = 8192
ALPHA = (2 * DEPTH) ** 0.25
LN_EPS = 1e-5
NORM_EPS = 1e-6
ROPE_THETA = 500000.0
IN_SIZES = (M_W, M_W, M_W, M_W, M_H, M_H, A_W, A_W, A_W, G_W, G_W, G_W, G_W, 3 * D)
OFFS = [0]
for _s in IN_SIZES:
    OFFS.append(OFFS[-1] + _s)
(O_MQ, O_MK, O_MV, O_MO, O_MI, O_MF, O_AQ, O_AK, O_AV, O_GQ, O_GF, O_GI, O_GG, O_GATE) = OFFS[:14]
IN_W = OFFS[14]
LN16 = math.log(1.0 / 16.0)

N_DMA_SEM = 30


class Buf:
    __slots__ = ("last_w", "readers", "excl")

    def __init__(self, excl=False):
        self.last_w = None
        self.readers = []
        self.excl = excl


class Op:
    __slots__ = ("eng", "fn", "deps", "is_dma", "inc", "val", "sem_i", "prev_val", "bar")

    def __init__(self, eng, fn, is_dma):
        self.eng = eng
        self.fn = fn
        self.deps = []
        self.is_dma = is_dma
        self.inc = False
        self.val = 0
        self.sem_i = -1
        self.prev_val = 0
        self.bar = False


class Rec:
    ENGS = ("pe", "act", "dve", "pool", "sp")
    DMA_SHARE = {"sp": list(range(0, 16)), "pool": list(range(16, 28)), "act": list(range(28, 30))}

    def __init__(self):
        self.ops = {e: [] for e in self.ENGS}
        self.dma_j = {e: 0 for e in self.ENGS}
        self.dma_cnt = [0] * N_DMA_SEM
        self.last_dma = [None] * N_DMA_SEM
        self.last_comp = {e: None for e in self.ENGS}

    def op(self, eng, fn, reads=(), writes=(), dma=False):
        o = Op(eng, fn, dma)
        deps = []
        for b in reads:
            if b.last_w is not None:
                deps.append((b.last_w, True))
            if b.excl:
                for r in b.readers:
                    if r.eng != eng:
                        deps.append((r, True))
        for b in writes:
            if b.last_w is not None:
                deps.append((b.last_w, True))
            for r in b.readers:
                deps.append((r, False))
        for b in reads:
            b.readers.append(o)
        for b in writes:
            b.last_w = o
            b.readers = []
        seen = set()
        for d, is_w in deps:
            if d is o or id(d) in seen:
                continue
            if (not d.is_dma) and (not dma) and d.eng == eng:
                if eng == "pe" or not is_w:
                    continue
            seen.add(id(d))
            o.deps.append(d)
            d.inc = True
        if dma:
            sh = self.DMA_SHARE[eng]
            si = sh[self.dma_j[eng] % len(sh)]
            self.dma_j[eng] += 1
            o.sem_i = si
            o.prev_val = self.dma_cnt[si]
            self.dma_cnt[si] += 16
            o.val = self.dma_cnt[si]
            self.last_dma[si] = o
        else:
            self.last_comp[eng] = o
        self.ops[eng].append(o)
        return o

    def barrier(self):
        o = Op("sp", lambda e: e.nop(), False)
        for e in self.ENGS:
            d = self.last_comp[e]
            if d is not None and e != "sp":
                d.inc = True
                o.deps.append(d)
        for d in self.last_dma:
            if d is not None:
                o.deps.append(d)
        o.inc = True
        self.ops["sp"].append(o)
        self.last_comp["sp"] = o
        for e in self.ENGS:
            if e == "sp":
                continue
            n = Op(e, lambda eo: eo.nop(), False)
            n.deps.append(o)
            self.ops[e].append(n)
            self.last_comp[e] = n
        return o

    def emit(self, nc, sems, dma_sems):
        engobj = {"pe": nc.tensor, "act": nc.scalar, "dve": nc.vector,
                  "pool": nc.gpsimd, "sp": nc.sync}
        for e in self.ENGS:
            c = 0
            for o in self.ops[e]:
                if (not o.is_dma) and o.inc:
                    c += 1
                    o.val = c
        n_wait = 0
        n_ins = 0
        for e in self.ENGS:
            eo = engobj[e]
            waited = {}
            for o in self.ops[e]:
                need = {}
                for d in o.deps:
                    key = ("d", d.sem_i) if d.is_dma else ("e", d.eng)
                    if d.val > need.get(key, 0):
                        need[key] = d.val
                if o.is_dma and o.prev_val > 0:
                    key = ("d", o.sem_i)
                    if o.prev_val > need.get(key, 0):
                        need[key] = o.prev_val
                for key, v in need.items():
                    if waited.get(key, 0) >= v:
                        continue
                    waited[key] = v
                    s = dma_sems[key[1]] if key[0] == "d" else sems[key[1]]
                    eo.wait_ge(s, v)
                    n_wait += 1
                ins = o.fn(eo)
                n_ins += 1
                if o.is_dma:
                    ins.then_inc(dma_sems[o.sem_i], 16)
                elif o.inc:
                    ins.then_inc(sems[e], 1)
        return n_ins, n_wait


class TL:
    def __init__(self, t, excl=False):
        self.t = t
        self.excl = excl
        self.b = Buf(excl)
        self.subs = {}

    def sub(self, key):
        s = self.subs.get(key)
        if s is None:
            s = Buf(self.excl)
            self.subs[key] = s
        return s

    def __getitem__(self, k):
        return self.t[k]


def _bl(x):
    out = []
    for v in x:
        if v is None:
            continue
        out.append(v.b if isinstance(v, TL) else v)
    return out


class Stage:
    def __init__(self, k):
        self.k = k
        self.es = contextlib.ExitStack()

    def __enter__(self):
        self.es.__enter__()
        return self

    def __exit__(self, *a):
        self.k.R.barrier()
        return self.es.__exit__(*a)

    def sb(self, name, shape, dt=F32):
        self.k.uid += 1
        return TL(self.es.enter_context(self.k.nc.sbuf_tensor("%s_%d" % (name, self.k.uid), list(shape), dt)))

    def ps(self, name, shape, dt=F32):
        self.k.uid += 1
        nbytes = int(np.prod(shape[1:])) * (2 if dt == BF16 else 4)
        assert nbytes % 2048 == 0, (name, shape)
        return TL(self.es.enter_context(self.k.nc.psum_tensor("%s_%d" % (name, self.k.uid), list(shape), dt)), excl=True)


CONST_NAMES = ["ident", "maskA", "rotT", "invf", "neg64", "M1", "Mref", "maskG", "blk2",
               "sel0", "sel1", "sel2", "sel3", "neg128"]

def host_consts():
    c = {}
    c["ident"] = np.eye(128, dtype=np.float32)
    k = np.arange(128)[:, None]
    q = np.arange(128)[None, :]
    c["maskA"] = (k <= q).astype(np.float32)
    rot = np.zeros((128, 128), np.float32)
    invf = np.zeros((128, 1), np.float32)
    inv = ROPE_THETA ** (-np.arange(0, 16, 2, dtype=np.float32) / 16.0)
    for p in range(128):
        d = p % 64
        if d < 8:
            rot[p + 8, p] = -1.0
            invf[p, 0] = inv[d]
        elif d < 16:
            rot[p - 8, p] = 1.0
            invf[p, 0] = inv[d - 8]
    c["rotT"] = rot
    c["invf"] = np.repeat(invf, 128, axis=1)
    s = np.arange(64)[:, None]
    t = np.arange(64)[None, :]
    neg = np.where(s > t, -30000.0, 0.0).astype(np.float32)
    c["neg64"] = np.pad(neg, ((0, 64), (0, 64)))
    ch = np.arange(128) // 64
    same = ch[:, None] == ch[None, :]
    Lm = (same & (k <= q)).astype(np.float32)
    ref = ch * 64 + 31
    c["M1"] = Lm - Lm[:, ref]
    mref = np.zeros((128, 128), np.float32)
    mref[:, 0] = Lm[:, 31]
    mref[:, 1] = Lm[:, 63] - Lm[:, 31]
    mref[:, 2] = Lm[:, 95]
    mref[:, 3] = Lm[:, 127] - Lm[:, 95]
    mref[:, 4] = Lm[:, 63]
    mref[:, 5] = Lm[:, 127]
    c["Mref"] = mref
    c["maskG"] = Lm
    blk = np.zeros((128, 128), np.float32)
    blk[:64, 0] = 1.0
    blk[64:, 1] = 1.0
    c["blk2"] = blk
    for hsel in range(4):
        sl = np.zeros((128, 128), np.float32)
        sl[hsel, :] = 1.0
        c["sel%d" % hsel] = sl
    c["neg128"] = ((Lm - 1.0) * 30000.0).astype(np.float32)
    names = list(CONST_NAMES)
    pack = np.concatenate([c[n] for n in names], axis=1).astype(np.float32)
    tt = np.arange(S)
    crow = np.zeros((1, 2 * S), np.float32)
    crow[0, :S] = (tt % 64 != 0).astype(np.float32)
    crow[0, S:] = ((tt % 64 == 0) & (tt > 0)).astype(np.float32)
    return names, pack, crow


class K:
    def __init__(self, nc, dbg=False):
        self.nc = nc
        self.R = Rec()
        self.uid = 0
        self.dbg = dbg
        L = DEPTH
        self.decl = {
            "x": ([S, D], F32), "positions": ([1, S], I32), "ln0_w": ([D], F32), "ln0_b": ([D], F32),
            "w_in": ([L, D, IN_W], F32), "m_conv_w": ([L, 4, 2 * M_W], F32), "m_conv_b": ([L, 2 * M_W], F32),
            "m_gate_b": ([L, 2 * M_H], F32), "m_norm_w": ([L, M_W], F32), "a_lambda": ([L, 4 * A_DH], F32),
            "a_norm_w": ([L, A_W], F32), "g_lb_logits": ([L, G_W], F32), "g_norm_w": ([L, G_W], F32),
            "p_m": ([L, M_W, D], F32), "p_a": ([L, A_W, D], F32), "p_g": ([L, G_W, D], F32),
            "w_out": ([L, D, D], F32), "ln1_w": ([L, D], F32), "ln1_b": ([L, D], F32),
            "w_up": ([L, D, D_FF], F32), "w_down": ([L, D_FF, D], F32), "ln2_w": ([L, D], F32),
            "ln2_b": ([L, D], F32), "consts": ([128, 128 * len(CONST_NAMES)], F32), "crow": ([1, 2 * S], F32),
        }
        self.declared = {}
        self.out = nc.dram_tensor("out", [S, D], F32, kind="ExternalOutput").ap()
        kind = "ExternalOutput" if dbg else "Internal"
        g = lambda name, shape, dt: nc.dram_tensor(name, list(shape), dt, kind=kind).ap()
        self.hres = g("hres", [S, D], F32)
        self.hT_d = g("hT_d", [D, S], BF16)
        self.mixT_d = g("mixT_d", [3 * 1024, S], BF16)
        self.yT_d = g("yT_d", [D, S], BF16)
        self.gsc = g("gsc", [8, S], F32)

    def __getattr__(self, name):
        decl = self.__dict__.get("decl", {})
        key = {"pos": "positions", "g_lb": "g_lb_logits"}.get(name, name)
        if key in decl:
            if key not in self.declared:
                shape, dt = decl[key]
                self.declared[key] = self.nc.dram_tensor(key, list(shape), dt, kind="ExternalInput").ap()
            return self.declared[key]
        raise AttributeError(name)

    def mm(self, out, lhsT, rhs, start, stop, r, w, skip=False):
        if skip:
            fn = lambda e: e.matmul(out, lhsT=lhsT, rhs=rhs, start=start, stop=stop, skip_group_check=True)
        else:
            fn = lambda e: e.matmul(out, lhsT=lhsT, rhs=rhs, start=start, stop=stop)
        return self.R.op("pe", fn, _bl(r), _bl(w))

    def tr(self, out, in_, ident, r, w):
        return self.R.op("pe", lambda e: e.transpose(out, in_, ident), _bl(r), _bl(w))

    def act(self, out, in_, func, r, w, bias=None, scale=None, accum=None):
        kw = {}
        if bias is not None:
            kw["bias"] = bias
        if scale is not None:
            kw["scale"] = scale
        if accum is not None:
            kw["accum_out"] = accum
        return self.R.op("act", lambda e: e.activation(out=out, in_=in_, func=func, **kw), _bl(r), _bl(w))

    def tt(self, eng, out, in0, in1, op, r, w):
        return self.R.op(eng, lambda e: e.tensor_tensor(out=out, in0=in0, in1=in1, op=op), _bl(r), _bl(w))

    def ts(self, eng, out, in0, s1, op0, r, w, s2=None, op1=None, accum=None):
        kw = {}
        if op1 is not None:
            kw["op1"] = op1
        if accum is not None:
            kw["accum_out"] = accum
        return self.R.op(eng, lambda e: e.tensor_scalar(out=out, in0=in0, scalar1=s1, scalar2=s2, op0=op0, **kw),
                         _bl(r), _bl(w))

    def stt(self, eng, out, in0, scalar, in1, op0, op1, r, w):
        return self.R.op(eng, lambda e: e.scalar_tensor_tensor(out=out, in0=in0, scalar=scalar, in1=in1,
                                                               op0=op0, op1=op1), _bl(r), _bl(w))

    def cp(self, eng, out, in_, r, w):
        return self.R.op(eng, lambda e: e.tensor_copy(out=out, in_=in_), _bl(r), _bl(w))

    def ms(self, eng, ap, val, w):
        return self.R.op(eng, lambda e: e.memset(ap, val), [], _bl(w))

    def recip(self, out, in_, r, w):
        return self.R.op("dve", lambda e: e.reciprocal(out=out, in_=in_), _bl(r), _bl(w))

    def rmax(self, out, in_, r, w):
        return self.R.op("dve", lambda e: e.reduce_max(out=out, in_=in_, axis=mybir.AxisListType.X), _bl(r), _bl(w))

    def rsum(self, out, in_, r, w):
        return self.R.op("dve", lambda e: e.reduce_sum(out=out, in_=in_, axis=mybir.AxisListType.X), _bl(r), _bl(w))

    def scan(self, out, d0, d1, init, op0, op1, r, w):
        return self.R.op("dve", lambda e: e.tensor_tensor_scan(out=out, data0=d0, data1=d1, initial=init,
                                                               op0=op0, op1=op1), _bl(r), _bl(w))

    def dma(self, eng, out, in_, r, w):
        return self.R.op(eng, lambda e: e.dma_start(out=out, in_=in_), _bl(r), _bl(w), dma=True)

    def dump(self, name, ap, shape, deps, dt=F32):
        if not self.dbg:
            return
        d = self.nc.dram_tensor("dbg_" + name, list(shape), dt, kind="ExternalOutput").ap()
        self.dma("sp", d, ap, deps, [])

    def setup_consts(self, es):
        nc = self.nc

        def sb(name, shape, dt=F32):
            return TL(es.enter_context(nc.sbuf_tensor(name, list(shape), dt)))
        ncst = len(CONST_NAMES)
        self.cst = sb("cst", [128, ncst * 128])
        self.dma("sp", self.cst[:], self.consts[:, :], [], [self.cst])
        self.c = {n: self.cst.t[:, i * 128:(i + 1) * 128] for i, n in enumerate(CONST_NAMES)}
        self.cstb = sb("cstb", [128, ncst * 128], BF16)
        self.cp("dve", self.cstb[:], self.cst[:], [self.cst], [self.cstb])
        self.cb = {n: self.cstb.t[:, i * 128:(i + 1) * 128] for i, n in enumerate(CONST_NAMES)}
        self.ones_f = sb("ones_f", [128, 128])
        self.ms("dve", self.ones_f[:], 1.0, [self.ones_f])
        self.ones_b = sb("ones_b", [128, 128], BF16)
        self.ms("dve", self.ones_b[:], 1.0, [self.ones_b])
        self.CR = [self.cst, self.cstb, self.ones_f, self.ones_b]

    def ln_alloc(self, st):
        a = {}
        a["stats"] = [st.sb("lnst", [128, 4, 6]) for _ in range(2)]
        a["mv"] = [st.sb("lnmv", [128, 2]) for _ in range(2)]
        a["sd"] = [st.sb("lnsd", [128, 4]) for _ in range(2)]
        a["xn"] = [st.sb("lnxn", [128, D]) for _ in range(2)]
        a["hh"] = [st.sb("lnhh", [128, D]) for _ in range(2)]
        a["hb"] = [st.sb("lnhb", [128, D], BF16) for _ in range(2)]
        a["hTs"] = [st.sb("lnhTs", [128, KC, 512], BF16) for _ in range(2)]
        a["pt"] = [st.ps("lnpt", [128, 8, 128], BF16) for _ in range(2)]
        a["cnt"] = 0
        return a

    def ln_block(self, a, z, zr, lnw, lnb, tb, res_out, hT_out=True):
        i = a["cnt"] % 2
        a["cnt"] += 1
        stats, mv, sd = a["stats"][i], a["mv"][i], a["sd"][i]
        for j in range(4):
            self.R.op("dve", (lambda e, j=j: e.bn_stats(out=stats.t[:, j, :], in_=z[:, j * 512:(j + 1) * 512])),
                      _bl(zr), _bl([stats]))
        self.R.op("dve", lambda e: e.bn_aggr(out=mv.t[:, :], in_=stats.t[:, :, :]), _bl([stats]), _bl([mv]))
        self.ts("dve", sd.t[:, 3:4], mv.t[:, 1:2], LN_EPS, ALU.add, [mv], [sd])
        self.act(sd.t[:, 0:1], sd.t[:, 3:4], AF.Sqrt, [sd], [sd])
        self.recip(sd.t[:, 1:2], sd.t[:, 0:1], [sd], [sd])
        self.stt("dve", sd.t[:, 2:3], mv.t[:, 0:1], -1.0, sd.t[:, 1:2], ALU.mult, ALU.mult, [mv, sd], [sd])
        xn = a["xn"][i]
        self.act(xn.t[:, :], z, AF.Identity, list(zr) + [sd], [xn], bias=sd.t[:, 2:3], scale=sd.t[:, 1:2])
        self.tt("dve", xn.t[:, :], xn.t[:, :], lnw.t[:, :], ALU.mult, [xn, lnw], [xn])
        hh = a["hh"][i]
        self.tt("dve", hh.t[:, :], xn.t[:, :], lnb.t[:, :], ALU.add, [xn, lnb], [hh])
        self.dma("sp", res_out[tb * 128:(tb + 1) * 128, :], hh.t[:, :], [hh], [])
        if not hT_out:
            return
        hb = a["hb"][i]
        self.act(hb.t[:, :], hh.t[:, :], AF.Copy, [hh], [hb])
        g4 = tb // 4
        hTs = a["hTs"][g4 % 2]
        for g in range(2):
            pt = a["pt"][g]
            for j in range(8):
                kc = g * 8 + j
                self.tr(pt.t[:, j, :], hb.t[:, kc * 128:(kc + 1) * 128], self.cb["ident"], [hb] + self.CR, [pt])
            eng = "dve" if g == 0 else "act"
            if eng == "dve":
                self.cp("dve", hTs.t[:, g * 8:(g + 1) * 8, (tb % 4) * 128:(tb % 4 + 1) * 128], pt.t[:, :, :], [pt], [hTs])
            else:
                self.act(hTs.t[:, g * 8:(g + 1) * 8, (tb % 4) * 128:(tb % 4 + 1) * 128], pt.t[:, :, :], AF.Copy, [pt], [hTs])
        if tb % 4 == 3:
            dst = self.hT_d.rearrange("(kc p) t -> p kc t", p=128)[:, :, g4 * 512:(g4 + 1) * 512]
            self.dma("sp", dst, hTs.t[:, :, :], [hTs], [])

    def load_bc(self, st, name, src_row, n):
        t = st.sb(name, [128, n])
        self.dma("sp", t.t[:, :], src_row.partition_broadcast(128), [], [t])
        return t

    def stage_ln0(self):
        with Stage(self) as st:
            lnw = self.load_bc(st, "lnw", self.ln0_w, D)
            lnb = self.load_bc(st, "lnb", self.ln0_b, D)
            a = self.ln_alloc(st)
            xin = [st.sb("xin", [128, D]) for _ in range(2)]
            ld = lambda tb: self.dma("sp", xin[tb % 2].t[:, :], self.x[tb * 128:(tb + 1) * 128, :], [], [xin[tb % 2]])
            ld(0)
            for tb in range(NT):
                z = xin[tb % 2]
                if tb + 1 < NT:
                    ld(tb + 1)
                self.ln_block(a, z.t[:, :], [z], lnw, lnb, tb, self.hres)

    def load_hT(self, st):
        hT = st.sb("hT", [128, KC, S], BF16)
        for kc in range(KC):
            self.dma("sp", hT.t[:, kc, :], self.hT_d[kc * 128:(kc + 1) * 128, :], [], [hT.sub(kc)])
        return hT

    def sin_table(self, st, out, ang, shift, tmp, kk):
        two_pi = 2.0 * math.pi
        self.ts("dve", tmp.t[:, :], ang.t[:, :], 1.0 / two_pi, ALU.mult, [ang], [tmp], s2=shift / two_pi, op1=ALU.add)
        self.cp("dve", kk.t[:, :], tmp.t[:, :], [tmp], [kk])
        self.cp("dve", tmp.t[:, :], kk.t[:, :], [kk], [tmp])
        self.stt("dve", tmp.t[:, :], tmp.t[:, :], -two_pi, ang.t[:, :], ALU.mult, ALU.add, [tmp, ang], [tmp])
        self.ts("dve", tmp.t[:, :], tmp.t[:, :], shift, ALU.add, [tmp], [tmp], s2=-math.pi, op1=ALU.max)
        self.ts("dve", tmp.t[:, :], tmp.t[:, :], math.pi, ALU.min, [tmp], [tmp])
        self.act(out.t[:, :], tmp.t[:, :], AF.Sin, [tmp], [out])

    def stage_attn(self, l):
        lam_init = 0.8 - 0.6 * math.exp(-0.3 * l)
        with Stage(self) as st:
            CR = self.CR
            hT = self.load_hT(st)
            Cs = st.sb("Cs", [128, S])
            Sn = st.sb("Sn", [128, S])
            with Stage(self) as s2:
                wif = s2.sb("wif", [128, KC, 8], BF16)
                self.dma("pool", wif.t[:, :, :], self.w_in[l][:, O_MI:O_MI + 8].rearrange("(kc p) c -> p kc c", p=128), [], [wif])
                g4 = s2.sb("g4", [4, 2, S])
                psG = s2.ps("psG", [128, 512])
                for which in range(2):
                    for tg in range(4):
                        for kc in range(KC):
                            self.mm(psG.t[0:4, :], wif.t[:, kc, which * 4:(which + 1) * 4], hT.t[:, kc, tg * 512:(tg + 1) * 512],
                                    kc == 0, kc == KC - 1, [wif, hT.sub(kc)], [psG])
                        self.act(g4.t[0:4, which, tg * 512:(tg + 1) * 512], psG.t[0:4, :], AF.Copy, [psG], [g4])
                self.dma("sp", self.gsc.rearrange("(w h) t -> h w t", h=4), g4.t[0:4, :, :], [g4], [])
                posi = s2.sb("posi", [128, S], I32)
                self.dma("sp", posi.t[:, :], self.pos[0].partition_broadcast(128), [], [posi])
                ang = s2.sb("ang", [128, S])
                tmpf = s2.sb("tmpf", [128, S])
                self.cp("dve", ang.t[:, :], posi.t[:, :], [posi], [ang])
                self.ts("dve", ang.t[:, :], ang.t[:, :], self.c["invf"][:, 0:1], ALU.mult, [ang] + CR, [ang])
                self.sin_table(s2, Sn, ang, 0.0, tmpf, posi)
                self.sin_table(s2, Cs, ang, math.pi / 2.0, tmpf, posi)
            lv = self.load_bc(st, "lv", self.a_lambda[l], 4 * A_DH)
            lsm = st.sb("lsm", [128, 8])
            lpr = st.sb("lpr", [128, 2 * A_DH])
            self.tt("dve", lpr.t[:, 0:64], lv.t[:, 0:64], lv.t[:, 64:128], ALU.mult, [lv], [lpr])
            self.tt("dve", lpr.t[:, 64:128], lv.t[:, 128:192], lv.t[:, 192:256], ALU.mult, [lv], [lpr])
            self.rsum(lsm.t[:, 0:1], lpr.t[:, 0:64], [lpr], [lsm])
            self.rsum(lsm.t[:, 1:2], lpr.t[:, 64:128], [lpr], [lsm])
            self.act(lsm.t[:, 2:4], lsm.t[:, 0:2], AF.Exp, [lsm], [lsm])
            self.tt("dve", lsm.t[:, 4:5], lsm.t[:, 3:4], lsm.t[:, 2:3], ALU.subtract, [lsm], [lsm])
            self.ts("dve", lsm.t[:, 5:6], lsm.t[:, 4:5], -lam_init, ALU.add, [lsm], [lsm])
            nlam = lsm.t[:, 5:6]
            anw = self.load_bc(st, "anw", self.a_norm_w[l], A_W)
            self.ts("dve", anw.t[:, :], anw.t[:, :], 1.0 - lam_init, ALU.mult, [anw], [anw])
            wts = [st.sb("awt", [128, KC, 384], BF16) for _ in range(2)]
            qT = [st.sb("aqT", [128, S], BF16) for _ in range(2)]
            kT = [st.sb("akT", [128, S], BF16) for _ in range(2)]
            va = [st.sb("ava", [128, NT, 132], BF16) for _ in range(2)]
            for p_ in range(2):
                self.ms("dve", va[p_].t[:, :, 128:132], 1.0, [va[p_]])
            sq = st.sb("asq", [128, S], BF16)
            qsb = [st.sb("aqs", [128, 512], BF16) for _ in range(2)]
            t1s = [st.sb("at1", [128, 512]) for _ in range(2)]
            t2s = [st.sb("at2", [128, 512]) for _ in range(2)]
            pTs = [st.sb("apT", [128, 512], BF16) for _ in range(3)]
            mx = st.sb("amx", [128, 16])
            dg = st.sb("adg", [128, 2])
            negm = [st.sb("anegm", [128, 2]) for _ in range(2)]
            tmp1 = st.sb("atmp1", [128, 4, 128])
            res = st.sb("ares", [128, 4, 128])
            rr = st.sb("arr", [128, 16])
            junk = st.sb("ajunk", [128, 128])
            hab = st.sb("ahab", [128, 4, 128], BF16)
            haT = [st.sb("ahaT", [128, S], BF16) for _ in range(2)]
            psA = [st.ps("apsA", [128, 512]) for _ in range(2)]
            psR = st.ps("apsR", [128, 512])
            psS = [st.ps("apsS", [128, 512]) for _ in range(2)]
            psO = st.ps("apsO", [128, 4, 256])
            psT = st.ps("apsT", [128, 8, 128], BF16)
            cA = 0
            cS = 0
            for h in range(A_H):
                p_ = h % 2
                w_ = wts[p_]
                for j, off in enumerate((O_AQ, O_AK, O_AV)):
                    src = self.w_in[l][:, off + h * 128: off + (h + 1) * 128].rearrange("(kc p) c -> p kc c", p=128)
                    self.dma("pool", w_.t[:, :, j * 128:(j + 1) * 128], src, [], [w_.sub(j)])
                for which, dst in enumerate((qT[p_], kT[p_])):
                    for tg in range(4):
                        pa = psA[cA % 2]
                        qs, t1, t2 = qsb[cA % 2], t1s[cA % 2], t2s[cA % 2]
                        cA += 1
                        for kc in range(KC):
                            self.mm(pa.t[:, :], w_.t[:, kc, which * 128:(which + 1) * 128],
                                    hT.t[:, kc, tg * 512:(tg + 1) * 512], kc == 0, kc == KC - 1,
                                    [w_.sub(which), hT.sub(kc)], [pa])
                        self.act(qs.t[:, :], pa.t[:, :], AF.Copy, [pa], [qs])
                        self.mm(psR.t[:, :], self.cb["rotT"], qs.t[:, :], True, True, [qs] + CR, [psR])
                        self.tt("dve", t1.t[:, :], qs.t[:, :], Cs.t[:, tg * 512:(tg + 1) * 512], ALU.mult, [qs, Cs], [t1])
                        self.tt("dve", t2.t[:, :], psR.t[:, :], Sn.t[:, tg * 512:(tg + 1) * 512], ALU.mult, [psR, Sn], [t2])
                        self.tt("pool", dst.t[:, tg * 512:(tg + 1) * 512], t1.t[:, :], t2.t[:, :], ALU.add, [t1, t2], [dst.sub(tg)])
                for tb in range(NT):
                    pa = psA[cA % 2]
                    cA += 1
                    for kc in range(KC):
                        self.mm(pa.t[:, 0:128], hT.t[:, kc, tb * 128:(tb + 1) * 128], w_.t[:, kc, 256:384],
                                kc == 0, kc == KC - 1, [w_.sub(2), hT.sub(kc)], [pa])
                    self.act(va[p_].t[:, tb, 0:128], pa.t[:, 0:128], AF.Copy, [pa], [va[p_].sub(tb)])
                for which, src in enumerate((qT[p_], kT[p_])):
                    self.tt("dve", sq.t[:, :], src.t[:, :], src.t[:, :], ALU.mult, [src.sub(t_) for t_ in range(4)], [sq])
                    for tg in range(4):
                        pa = psA[cA % 2]
                        cA += 1
                        self.mm(pa.t[0:2, :], self.cb["blk2"][:, 0:2], sq.t[:, tg * 512:(tg + 1) * 512], True, True, [sq] + CR, [pa])
                        self.rmax(mx.t[0:2, which * 4 + tg: which * 4 + tg + 1], pa.t[0:2, :], [pa], [mx])
                self.rmax(mx.t[0:2, 8:9], mx.t[0:2, 0:4], [mx], [mx])
                self.rmax(mx.t[0:2, 9:10], mx.t[0:2, 4:8], [mx], [mx])
                self.tt("dve", mx.t[0:2, 10:11], mx.t[0:2, 8:9], mx.t[0:2, 9:10], ALU.mult, [mx], [mx])
                self.act(mx.t[0:2, 11:12], mx.t[0:2, 10:11], AF.Sqrt, [mx], [mx])
                self.ts("dve", mx.t[0:2, 12:13], mx.t[0:2, 11:12], -0.125, ALU.mult, [mx], [mx])
                self.ts("dve", dg.t[0:2, 0:2], self.c["ident"][0:2, 0:2], mx.t[0:2, 12:13], ALU.mult, [mx] + CR, [dg])
                pa = psA[cA % 2]
                cA += 1
                self.mm(pa.t[:, 0:2], self.ones_f.t[0:2, :], dg.t[0:2, 0:2], True, True, [dg] + CR, [pa])
                ng = negm[p_]
                self.cp("dve", ng.t[:, :], pa.t[:, 0:2], [pa], [ng])
                qd = [qT[p_].sub(t_) for t_ in range(4)]
                kd = [kT[p_].sub(t_) for t_ in range(4)]
                for g in range(4):
                    for n in range(2):
                        self.ms("dve", psO.t[:, :, :], 0.0, [psO])
                        for j in range(4 * g + 4):
                            q0 = max(j * 128, g * 512)
                            qn = (g + 1) * 512 - q0
                            pst = psS[cS % 2]
                            pT = pTs[cS % 3]
                            cS += 1
                            self.mm(pst.t[:, 0:qn], kT[p_].t[n * 64:(n + 1) * 64, j * 128:(j + 1) * 128],
                                    qT[p_].t[n * 64:(n + 1) * 64, q0:q0 + qn], True, True, qd + kd, [pst])
                            self.act(pT.t[:, 0:qn], pst.t[:, 0:qn], AF.Exp, [pst, ng], [pT],
                                     bias=ng.t[:, n:n + 1], scale=0.125)
                            if j * 128 >= g * 512:
                                self.tt("pool", pT.t[:, 0:128], pT.t[:, 0:128], self.cb["maskA"], ALU.mult, [pT] + CR, [pT])
                            for b in range(4):
                                if g * 4 + b < j:
                                    continue
                                c0 = g * 512 + b * 128 - q0
                                self.mm(psO.t[:, b, 0:129], pT.t[:, c0:c0 + 128], va[p_].t[:, j, 0:129],
                                        False, False, [pT, va[p_].sub(j)], [psO], skip=True)
                        if n == 0:
                            self.recip(rr.t[:, 0:4], psO.t[:, :, 128], [psO], [rr])
                            for b in range(4):
                                self.act(tmp1.t[:, b, :], psO.t[:, b, 0:128], AF.Copy, [psO, rr], [tmp1], scale=rr.t[:, b:b + 1])
                        else:
                            self.recip(rr.t[:, 4:8], psO.t[:, :, 128], [psO], [rr])
                            self.ts("dve", rr.t[:, 4:8], rr.t[:, 4:8], nlam, ALU.mult, [rr, lsm], [rr])
                            for b in range(4):
                                self.stt("dve", res.t[:, b, :], psO.t[:, b, 0:128], rr.t[:, 4 + b:5 + b], tmp1.t[:, b, :],
                                         ALU.mult, ALU.add, [psO, rr, tmp1], [res])
                    for b in range(4):
                        self.act(junk.t[:, :], res.t[:, b, :], AF.Square, [res], [junk, rr], accum=rr.t[:, 8 + b:9 + b])
                    self.ts("dve", rr.t[:, 12:16], rr.t[:, 8:12], 1.0 / 128.0, ALU.mult, [rr], [rr], s2=NORM_EPS, op1=ALU.add)
                    self.act(rr.t[:, 12:16], rr.t[:, 12:16], AF.Ln, [rr], [rr])
                    self.act(rr.t[:, 12:16], rr.t[:, 12:16], AF.Exp, [rr], [rr], scale=-0.5)
                    for b in range(4):
                        self.stt("dve", hab.t[:, b, :], res.t[:, b, :], rr.t[:, 12 + b:13 + b], anw.t[:, h * 128:(h + 1) * 128],
                                 ALU.mult, ALU.mult, [res, rr, anw], [hab])
                        self.tr(psT.t[:, b, :], hab.t[:, b, :], self.cb["ident"], [hab] + CR, [psT])
                    self.act(haT[p_].t[:, g * 512:(g + 1) * 512], psT.t[:, 0:4, :], AF.Copy, [psT], [haT[p_]])
                self.dma("sp", self.mixT_d[1024 + h * 128: 1024 + (h + 1) * 128, :], haT[p_].t[:, :], [haT[p_]], [])

    def stage_mlstm(self, l):
        CR = self.CR
        with Stage(self) as outer:
            tok4 = outer.sb("mtok4", [128, NT, 16])
            decs = outer.sb("mdecs", [128, M_H, 32])
            nM = outer.sb("mnM", [4, S])
            with Stage(self) as st:
                gi = st.sb("gi", [4, S])
                gf = st.sb("gf", [4, S])
                tb_ = st.sb("gb", [4, S])
                tM = st.sb("gM", [4, S])
                tP = st.sb("gP", [4, S])
                te = st.sb("ge", [4, S])
                tw = st.sb("gw", [4, S])
                keep = st.sb("gkeep", [4, S])
                strt = st.sb("gstrt", [4, S])
                gb = st.sb("ggb", [4, 4])
                self.dma("sp", gi.t[0:4, :], self.gsc[0:4, :], [], [gi])
                self.dma("sp", gf.t[0:4, :], self.gsc[4:8, :], [], [gf])
                self.dma("sp", keep.t[0:4, :], self.crow[0, 0:S].partition_broadcast(4), [], [keep])
                self.dma("sp", strt.t[0:4, :], self.crow[0, S:2 * S].partition_broadcast(4), [], [strt])
                self.dma("sp", gb.t[0:4, 0:1], self.m_gate_b[l][0:4].rearrange("(h o) -> h o", o=1), [], [gb])
                self.dma("sp", gb.t[0:4, 1:2], self.m_gate_b[l][4:8].rearrange("(h o) -> h o", o=1), [], [gb])
                self.ts("dve", gb.t[0:4, 2:3], gb.t[0:4, 1:2], -1.0, ALU.mult, [gb], [gb])
                self.ts("dve", gi.t[0:4, :], gi.t[0:4, :], gb.t[0:4, 0:1], ALU.add, [gi, gb], [gi])
                self.act(gf.t[0:4, :], gf.t[0:4, :], AF.Exp, [gf, gb], [gf], bias=gb.t[0:4, 2:3], scale=-1.0)
                self.act(gf.t[0:4, :], gf.t[0:4, :], AF.Ln, [gf], [gf], bias=1.0)
                self.ts("dve", gf.t[0:4, :], gf.t[0:4, :], -1.0, ALU.mult, [gf], [gf])
                self.scan(tb_.t[0:4, :], keep.t[0:4, :], gf.t[0:4, :], 0.0, ALU.mult, ALU.add, [keep, gf], [tb_])
                self.tt("dve", gi.t[0:4, :], gi.t[0:4, :], tb_.t[0:4, :], ALU.subtract, [gi, tb_], [gi])
                self.ms("dve", gf.t[0:4, 0:1], 0.0, [gf])
                self.tt("dve", gf.t[0:4, 1:S], tb_.t[0:4, 0:S - 1], strt.t[0:4, 1:S], ALU.mult, [tb_, strt], [gf])
                self.scan(tM.t[0:4, :], gf.t[0:4, :], gi.t[0:4, :], 0.0, ALU.add, ALU.max, [gf, gi], [tM])
                self.tt("dve", gf.t[0:4, 1:S], tb_.t[0:4, 0:S - 1], tM.t[0:4, 0:S - 1], ALU.add, [tb_, tM], [gf])
                self.tt("dve", gf.t[0:4, 1:S], gf.t[0:4, 1:S], strt.t[0:4, 1:S], ALU.mult, [gf, strt], [gf])
                self.scan(tP.t[0:4, :], keep.t[0:4, :], gf.t[0:4, :], 0.0, ALU.mult, ALU.add, [keep, gf], [tP])
                self.tt("dve", tP.t[0:4, :], tP.t[0:4, :], tM.t[0:4, :], ALU.subtract, [tP, tM], [tP])
                self.act(tP.t[0:4, :], tP.t[0:4, :], AF.Exp, [tP], [tP])
                self.tt("dve", te.t[0:4, :], tb_.t[0:4, :], tM.t[0:4, :], ALU.add, [tb_, tM], [te])
                self.act(te.t[0:4, :], te.t[0:4, :], AF.Exp, [te], [te], scale=-1.0)
                self.ts("dve", gi.t[0:4, :], gi.t[0:4, :], LN16, ALU.add, [gi], [gi])
                M3 = tM.t[0:4, :].rearrange("p (c t) -> p c t", t=64)
                self.tt("dve", tw.t[0:4, :].rearrange("p (c t) -> p c t", t=64), gi.t[0:4, :].rearrange("p (c t) -> p c t", t=64),
                        M3[:, :, 63:64].to_broadcast([4, 32, 64]), ALU.subtract, [gi, tM], [tw])
                self.act(tw.t[0:4, :], tw.t[0:4, :], AF.Exp, [tw], [tw])
                self.ts("dve", nM.t[0:4, :], tM.t[0:4, :], -1.0, ALU.mult, [tM], [nM])
                pq = st.ps("gpq", [128, NT, 32])
                for tb in range(NT):
                    for qi, src in enumerate((gi, tw, tP, te)):
                        self.mm(pq.t[:, tb, qi * 4:(qi + 1) * 4], src.t[0:4, tb * 128:(tb + 1) * 128], self.c["ident"][0:4, 0:4],
                                True, True, [src] + CR, [pq])
                self.cp("dve", tok4.t[:, :, :], pq.t[:, :, 0:16], [pq], [tok4])
                pd = st.ps("gpd", [128, M_H, 128])
                wl = tP.t[0:4, :].rearrange("p (c t) -> p c t", t=64)[:, :, 63]
                for hd in range(M_H):
                    self.mm(pd.t[:, hd, 0:32], self.c["sel%d" % hd][0:4, :], wl, True, True, [tP] + CR, [pd])
                self.cp("dve", decs.t[:, :, :], pd.t[:, :, 0:32], [pd], [decs])
            with Stage(self) as st:
                hT = self.load_hT(st)
                mnw = self.load_bc(st, "mnw", self.m_norm_w[l], M_W)
                cwT = st.sb("cwT", [8, 2 * M_W])
                self.dma("sp", cwT.t[0:4, :], self.m_conv_w[l], [], [cwT])
                self.dma("sp", cwT.t[4:5, :], self.m_conv_b[l:l + 1, :], [], [cwT])
                cw = st.sb("cw", [128, 16, 8])
                with Stage(self) as s2:
                    pc = s2.ps("mpc", [128, 16, 32])
                    for ch in range(16):
                        self.tr(pc.t[:, ch, 0:5], cwT.t[0:5, ch * 128:(ch + 1) * 128], self.c["ident"][0:5, 0:5], [cwT] + CR, [pc])
                    self.cp("dve", cw.t[:, :, 0:5], pc.t[:, :, 0:5], [pc], [cw])
                wqk = [st.sb("mwqk", [128, KC, 128], BF16) for _ in range(2)]
                wvo = st.sb("mwvo", [128, KC, 512], BF16)
                xpad = st.sb("mxpad", [128, S + 4])
                self.ms("dve", xpad.t[:, 0:3], 0.0, [xpad])
                cacc = st.sb("mcacc", [128, S])
                qz = [st.sb("mqz0", [128, 2, S], BF16), st.sb("mqz1", [128, 2, S], BF16)]
                qk = [qz[0], st.sb("mkT", [128, 2, S], BF16)]
                vaug = st.sb("mvaug", [128, NT, 260], BF16)
                self.ms("dve", vaug.t[:, :, 256:260], 1.0, [vaug])
                og = st.sb("mog", [128, NT, 256], BF16)
                hmT = st.sb("mhmT", [128, 2, S], BF16)
                Cst = st.sb("mC", [128, 2, 260])
                Cb = [st.sb("mCb", [128, 2, 260], BF16) for _ in range(2)]
                DT = [st.sb("mDT", [128, 128]) for _ in range(2)]
                ST = [st.sb("mST", [128, 128], BF16) for _ in range(2)]
                kws = [st.sb("mkws", [128, 2, 128], BF16) for _ in range(2)]
                Bs = [st.sb("mBs", [128, 260]) for _ in range(2)]
                nd = [st.sb("mnd", [128, 260]) for _ in range(2)]
                sm = [st.sb("msm", [128, 8]) for _ in range(2)]
                junk = st.sb("mjunk", [128, 256])
                tmph = [st.sb("mtmph", [128, 256]) for _ in range(2)]
                hmb = [st.sb("mhmb", [128, 256], BF16) for _ in range(2)]
                psA = [st.ps("mpsA", [128, 512]) for _ in range(2)]
                psSB = st.ps("mpsSB", [128, 512])
                psN = st.ps("mpsN", [128, 512])
                psI = st.ps("mpsI", [128, 512])
                psK = st.ps("mpsK", [128, 1024], BF16)
                psV = st.ps("mpsV", [128, 2, 512])
                cA = 0
                cW = 0
                for hd in range(M_H):
                    for j, off in enumerate((O_MV, O_MO)):
                        src = self.w_in[l][:, off + hd * 256: off + (hd + 1) * 256].rearrange("(kc p) c -> p kc c", p=128)
                        self.dma("pool", wvo.t[:, :, j * 256:(j + 1) * 256], src, [], [wvo.sub(j)])
                    for which, off in enumerate((O_MQ, O_MK)):
                        for dc in range(2):
                            w_ = wqk[cW % 2]
                            cW += 1
                            src = self.w_in[l][:, off + hd * 256 + dc * 128: off + hd * 256 + (dc + 1) * 128].rearrange("(kc p) c -> p kc c", p=128)
                            self.dma("pool", w_.t[:, :, :], src, [], [w_])
                            ch = which * 8 + hd * 2 + dc
                            for tg in range(4):
                                pa = psA[cA % 2]
                                cA += 1
                                for kc in range(KC):
                                    self.mm(pa.t[:, :], w_.t[:, kc, :], hT.t[:, kc, tg * 512:(tg + 1) * 512], kc == 0, kc == KC - 1,
                                            [w_, hT.sub(kc)], [pa])
                                self.act(xpad.t[:, 3 + tg * 512: 3 + (tg + 1) * 512], pa.t[:, :], AF.Copy, [pa], [xpad])
                            self.ts("dve", cacc.t[:, :], xpad.t[:, 3:3 + S], cw.t[:, ch, 3:4], ALU.mult, [xpad, cw], [cacc],
                                    s2=cw.t[:, ch, 4:5], op1=ALU.add)
                            for j in range(3):
                                self.stt("dve", cacc.t[:, :], xpad.t[:, j:j + S], cw.t[:, ch, j:j + 1], cacc.t[:, :], ALU.mult, ALU.add,
                                         [xpad, cw, cacc], [cacc])
                            self.act(qk[which].t[:, dc, :], cacc.t[:, :], AF.Silu, [cacc], [qk[which]])
                    self.cp("pool", qz[1].t[:, :, :], qz[0].t[:, :, :], [qz[0]], [qz[1]])
                    for z in range(2):
                        zv = qz[z].t[:, :, :].rearrange("p d (b c t) -> p d b c t", c=2, t=64)[:, :, :, 1 - z, :]
                        self.ms("pool", zv, 0.0, [qz[z]])
                    for tb in range(NT):
                        pa = psA[cA % 2]
                        cA += 1
                        for kc in range(KC):
                            self.mm(pa.t[:, :], hT.t[:, kc, tb * 128:(tb + 1) * 128], wvo.t[:, kc, :], kc == 0, kc == KC - 1,
                                    [wvo.sub(0), wvo.sub(1), hT.sub(kc)], [pa])
                        self.act(vaug.t[:, tb, 0:256], pa.t[:, 0:256], AF.Copy, [pa], [vaug.sub(tb)])
                        self.act(og.t[:, tb, :], pa.t[:, 256:512], AF.Sigmoid, [pa], [og.sub(tb)])
                    self.ms("dve", Cst.t[:, :, :], 0.0, [Cst])
                    self.ms("dve", Cb[0].t[:, :, :], 0.0, [Cb[0]])
                    cbi = 0
                    for tb in range(NT):
                        i2 = tb % 2
                        tsl = slice(tb * 128, (tb + 1) * 128)
                        for z in range(2):
                            for dc in range(2):
                                self.mm(psSB.t[:, 0:128], qk[1].t[:, dc, tsl], qz[z].t[:, dc, tsl], z == 0 and dc == 0, z == 1 and dc == 1,
                                        [qz[z], qk[1]], [psSB])
                        self.mm(psSB.t[:, 128:256], self.c["sel%d" % hd][0:4, :], nM.t[0:4, tsl], True, False, [nM] + CR, [psSB])
                        self.mm(psSB.t[:, 128:256], self.c["ident"], self.c["neg128"], False, True, CR, [psSB])
                        self.act(DT[i2].t[:, :], psSB.t[:, 128:256], AF.Exp, [psSB, tok4], [DT[i2]], bias=tok4.t[:, tb, hd:hd + 1])
                        self.tt("dve", ST[i2].t[:, :], psSB.t[:, 0:128], DT[i2].t[:, :], ALU.mult, [psSB, DT[i2]], [ST[i2]])
                        self.mm(psN.t[:, 0:257], ST[i2].t[:, :], vaug.t[:, tb, 0:257], True, True, [ST[i2], vaug.sub(tb)], [psN])
                        for dc in range(2):
                            self.tr(psK.t[:, dc * 128:(dc + 1) * 128], qk[1].t[:, dc, tsl], self.cb["ident"], [qk[1]] + CR, [psK])
                        self.act(kws[i2].t[:, :, :], psK.t[:, 0:256].rearrange("p (d e) -> p d e", e=128), AF.Copy, [psK, tok4], [kws[i2]],
                                 scale=tok4.t[:, tb, 4 + hd:5 + hd])
                        for c2 in range(2):
                            cb_cur = Cb[cbi % 2]
                            for dc in range(2):
                                self.mm(psI.t[:, 0:257], qz[c2].t[:, dc, tsl], cb_cur.t[:, dc, 0:257], c2 == 0 and dc == 0, c2 == 1 and dc == 1,
                                        [qz[c2], cb_cur], [psI])
                            prt = slice(c2 * 64, (c2 + 1) * 64)
                            for dc in range(2):
                                self.mm(psV.t[:, dc, 0:257], kws[i2].t[prt, dc, :], vaug.t[prt, tb, 0:257], True, True,
                                        [kws[i2], vaug.sub(tb)], [psV.sub(dc)])
                            cb_nxt = Cb[(cbi + 1) % 2]
                            for dc in range(2):
                                self.stt("dve", Cst.t[:, dc, 0:257], Cst.t[:, dc, 0:257], decs.t[:, hd, tb * 2 + c2: tb * 2 + c2 + 1],
                                         psV.t[:, dc, 0:257], ALU.mult, ALU.add, [Cst, decs, psV.sub(dc)], [Cst])
                            self.act(cb_nxt.t[:, :, 0:257], Cst.t[:, :, 0:257], AF.Copy, [Cst], [cb_nxt])
                            cbi += 1
                        self.act(Bs[i2].t[:, 0:257], psI.t[:, 0:257], AF.Copy, [psI, tok4], [Bs[i2]], scale=tok4.t[:, tb, 8 + hd:9 + hd])
                        self.tt("dve", nd[i2].t[:, 0:257], psN.t[:, 0:257], Bs[i2].t[:, 0:257], ALU.add, [psN, Bs[i2]], [nd[i2]])
                        s_ = sm[i2]
                        self.act(s_.t[:, 6:7], nd[i2].t[:, 256:257], AF.Abs, [nd[i2]], [s_])
                        self.ts("dve", s_.t[:, 0:1], s_.t[:, 6:7], tok4.t[:, tb, 12 + hd:13 + hd], ALU.max, [s_, tok4], [s_])
                        self.recip(s_.t[:, 1:2], s_.t[:, 0:1], [s_], [s_])
                        self.act(junk.t[:, :], nd[i2].t[:, 0:256], AF.Square, [nd[i2], s_], [junk, s_], scale=s_.t[:, 1:2], accum=s_.t[:, 2:3])
                        self.ts("dve", s_.t[:, 3:4], s_.t[:, 2:3], 1.0 / 256.0, ALU.mult, [s_], [s_], s2=NORM_EPS, op1=ALU.add)
                        self.act(s_.t[:, 3:4], s_.t[:, 3:4], AF.Ln, [s_], [s_])
                        self.act(s_.t[:, 4:5], s_.t[:, 3:4], AF.Exp, [s_], [s_], scale=-0.5)
                        self.tt("dve", s_.t[:, 5:6], s_.t[:, 4:5], s_.t[:, 1:2], ALU.mult, [s_], [s_])
                        self.stt("dve", tmph[i2].t[:, :], nd[i2].t[:, 0:256], s_.t[:, 5:6], mnw.t[:, hd * 256:(hd + 1) * 256], ALU.mult, ALU.mult,
                                 [nd[i2], s_, mnw], [tmph[i2]])
                        self.tt("pool", hmb[i2].t[:, :], tmph[i2].t[:, :], og.t[:, tb, :], ALU.mult, [tmph[i2], og.sub(tb)], [hmb[i2]])
                        for dc in range(2):
                            self.tr(psK.t[:, 256 + dc * 128: 256 + (dc + 1) * 128], hmb[i2].t[:, dc * 128:(dc + 1) * 128], self.cb["ident"],
                                    [hmb[i2]] + CR, [psK])
                        self.cp("dve", hmT.t[:, :, tsl], psK.t[:, 256:512].rearrange("p (d e) -> p d e", e=128), [psK], [hmT])
                    for dc in range(2):
                        self.dma("sp", self.mixT_d[hd * 256 + dc * 128: hd * 256 + (dc + 1) * 128, :], hmT.t[:, dc, :], [hmT], [])

    def stage_hgrn(self, l):
        CR = self.CR
        with Stage(self) as st:
            hT = self.load_hT(st)
            gnw = self.load_bc(st, "gnw", self.g_norm_w[l], G_W)
            lb = st.sb("glb", [128, G_W])
            oml = st.sb("goml", [128, G_W])
            with Stage(self) as s2:
                lg = [self.load_bc(s2, "glg", self.g_lb[j], G_W) for j in range(DEPTH)]
                mxl = s2.sb("gmxl", [128, G_W])
                sme = s2.sb("gsme", [128, G_W])
                self.tt("dve", mxl.t[:, :], lg[0].t[:, :], lg[1].t[:, :], ALU.max, [lg[0], lg[1]], [mxl])
                for j in range(DEPTH):
                    self.tt("dve", lg[j].t[:, :], lg[j].t[:, :], mxl.t[:, :], ALU.subtract, [lg[j], mxl], [lg[j]])
                    self.act(lg[j].t[:, :], lg[j].t[:, :], AF.Exp, [lg[j]], [lg[j]])
                self.tt("dve", sme.t[:, :], lg[0].t[:, :], lg[1].t[:, :], ALU.add, [lg[0], lg[1]], [sme])
                self.recip(sme.t[:, :], sme.t[:, :], [sme], [sme])
                for j in range(DEPTH):
                    self.tt("dve", lg[j].t[:, :], lg[j].t[:, :], sme.t[:, :], ALU.mult, [lg[j], sme], [lg[j]])
                self.cp("dve", mxl.t[:, :], lg[0].t[:, :], [lg[0]], [mxl])
                for j in range(1, l + 1):
                    self.tt("dve", mxl.t[:, :], mxl.t[:, :], lg[j].t[:, :], ALU.add, [mxl, lg[j]], [mxl])
                self.tt("dve", lb.t[:, :], mxl.t[:, :], lg[0].t[:, :], ALU.subtract, [mxl, lg[0]], [lb])
                self.ts("dve", oml.t[:, :], lb.t[:, :], -1.0, ALU.mult, [lb], [oml], s2=1.0, op1=ALU.add)
            wg = [st.sb("gwg", [128, KC, 512], BF16) for _ in range(2)]
            f32t = lambda nm: [st.sb(nm, [128, 128]) for _ in range(2)]
            bft = lambda nm: [st.sb(nm, [128, 128], BF16) for _ in range(2)]
            tq, tf, tlf, tk, tgg, teq, tek = (f32t("gq"), f32t("gf"), f32t("glf"), f32t("gk"), f32t("ggg"), f32t("geq"), f32t("gek"))
            tqt, tkt, tvb, tam, tktT, thgb = (bft("gqt"), bft("gkt"), bft("gvb"), bft("gam"), bft("gktT"), bft("ghgb"))
            qz0 = bft("gqz0")
            qz1 = bft("gqz1")
            for i in range(2):
                self.ms("dve", qz0[i].t[:, :], 0.0, [qz0[i]])
                self.ms("dve", qz1[i].t[:, :], 0.0, [qz1[i]])
            tE = [st.sb("gE", [128, 8]) for _ in range(2)]
            tsm = [st.sb("gsm", [128, 8]) for _ in range(2)]
            ttmp = f32t("gtmp")
            junk = st.sb("gjunk", [128, 128])
            Sst = st.sb("gS", [128, 128])
            Sdec = st.sb("gSdec", [128, 128])
            Sb = [st.sb("gSb", [128, 128], BF16) for _ in range(2)]
            hgT = [st.sb("ghgT", [128, S], BF16) for _ in range(2)]
            psA = [st.ps("gpsA", [128, 512]) for _ in range(2)]
            psB = st.ps("gpsB", [128, 512])
            psT2 = st.ps("gpsT2", [128, 8, 128], BF16)
            psT = st.ps("gpsT", [128, 8, 128], BF16)
            psAt = st.ps("gpsAt", [128, 512])
            psO = st.ps("gpsO", [128, 512])
            psKV = st.ps("gpsKV", [128, 512])
            cA = 0
            import os as _os
            _nh = int(_os.environ.get("HG_HEADS", G_H)); _nb = int(_os.environ.get("HG_BLOCKS", NT)); _lvl = int(_os.environ.get("HG_LVL", 9))
            for h in range(_nh):
                w_ = wg[h % 2]
                for j, off in enumerate((O_GQ, O_GF, O_GI, O_GG)):
                    src = self.w_in[l][:, off + h * 128: off + (h + 1) * 128].rearrange("(kc p) c -> p kc c", p=128)
                    self.dma("pool", w_.t[:, :, j * 128:(j + 1) * 128], src, [], [w_.sub(j)])
                wdeps = [w_.sub(j) for j in range(4)]
                hs = slice(h * 128, (h + 1) * 128)
                self.ms("dve", Sst.t[:, :], 0.0, [Sst])
                def phaseA(tb, h=h, w_=w_, wdeps=wdeps, hs=hs):
                    nonlocal cA
                    i = tb % 2
                    pa = psA[cA % 2]
                    cA += 1
                    for kc in range(KC):
                        self.mm(pa.t[:, :], hT.t[:, kc, tb * 128:(tb + 1) * 128], w_.t[:, kc, :], kc == 0, kc == KC - 1,
                                wdeps + [hT.sub(kc)], [pa])
                    self.act(teq[i].t[:, :], pa.t[:, 0:128], AF.Exp, [pa], [teq[i]], scale=-1.0)
                    self.act(tf[i].t[:, :], pa.t[:, 128:256], AF.Exp, [pa], [tf[i]], scale=-1.0)
                    self.act(tvb[i].t[:, :], pa.t[:, 256:384], AF.Copy, [pa], [tvb[i]])
                    self.act(tek[i].t[:, :], pa.t[:, 384:512], AF.Exp, [pa], [tek[i]], scale=-1.0)
                    self.ts("dve", teq[i].t[:, :], teq[i].t[:, :], 1.0, ALU.add, [teq[i]], [teq[i]])
                    self.recip(teq[i].t[:, :], teq[i].t[:, :], [teq[i]], [teq[i]])
                    self.tt("dve", tq[i].t[:, :], pa.t[:, 0:128], teq[i].t[:, :], ALU.mult, [pa, teq[i]], [tq[i]])
                    self.ts("dve", tek[i].t[:, :], tek[i].t[:, :], 1.0, ALU.add, [tek[i]], [tek[i]])
                    self.recip(tek[i].t[:, :], tek[i].t[:, :], [tek[i]], [tek[i]])
                    self.tt("dve", tgg[i].t[:, :], pa.t[:, 384:512], tek[i].t[:, :], ALU.mult, [pa, tek[i]], [tgg[i]])
                    self.ts("dve", tf[i].t[:, :], tf[i].t[:, :], 1.0, ALU.add, [tf[i]], [tf[i]])
                    self.recip(tf[i].t[:, :], tf[i].t[:, :], [tf[i]], [tf[i]])
                    self.tt("dve", tf[i].t[:, :], tf[i].t[:, :], oml.t[:, hs], ALU.mult, [tf[i], oml], [tf[i]])
                    self.tt("dve", tf[i].t[:, :], tf[i].t[:, :], lb.t[:, hs], ALU.add, [tf[i], lb], [tf[i]])
                    self.ts("dve", tf[i].t[:, :], tf[i].t[:, :], 1e-30, ALU.max, [tf[i]], [tf[i]])
                    self.act(tlf[i].t[:, :], tf[i].t[:, :], AF.Ln, [tf[i]], [tlf[i]])
                    self.ts("dve", tk[i].t[:, :], tf[i].t[:, :], -1.0, ALU.mult, [tf[i]], [tk[i]], s2=1.0, op1=ALU.add)
                    self.mm(psB.t[:, 0:128], self.c["M1"], tlf[i].t[:, :], True, True, [tlf[i]] + CR, [psB])
                    self.mm(psB.t[:, 128:136], tlf[i].t[:, :], self.c["Mref"][:, 0:8], True, True, [tlf[i]] + CR, [psB])
                    self.act(teq[i].t[:, :], psB.t[:, 0:128], AF.Exp, [psB], [teq[i]])
                    self.act(tek[i].t[:, :], psB.t[:, 0:128], AF.Exp, [psB], [tek[i]], scale=-1.0)
                    self.act(tE[i].t[:, 0:8], psB.t[:, 128:136], AF.Exp, [psB], [tE[i]])
                    self.tt("dve", tqt[i].t[:, :], tq[i].t[:, :], teq[i].t[:, :], ALU.mult, [tq[i], teq[i]], [tqt[i]])
                    self.tt("dve", tkt[i].t[:, :], tk[i].t[:, :], tek[i].t[:, :], ALU.mult, [tk[i], tek[i]], [tkt[i]])
                    self.tr(psT.t[:, 0, :], tqt[i].t[:, :], self.cb["ident"], [tqt[i]] + CR, [psT])
                    self.tr(psT.t[:, 1, :], tkt[i].t[:, :], self.cb["ident"], [tkt[i]] + CR, [psT])
                    self.cp("dve", qz0[i].t[:, 0:64], psT.t[:, 0, 0:64], [psT], [qz0[i]])
                    self.cp("dve", qz1[i].t[:, 64:128], psT.t[:, 0, 64:128], [psT], [qz1[i]])
                    self.act(tktT[i].t[:, :], psT.t[:, 1, :], AF.Copy, [psT], [tktT[i]])

                def phaseB(tb, h=h, hs=hs):
                    i = tb % 2
                    self.mm(psAt.t[:, 0:128], tktT[i].t[:, :], qz0[i].t[:, :], True, False, [tktT[i], qz0[i]], [psAt])
                    self.mm(psAt.t[:, 0:128], tktT[i].t[:, :], qz1[i].t[:, :], False, True, [tktT[i], qz1[i]], [psAt])
                    self.tt("dve", tam[i].t[:, :], psAt.t[:, 0:128], self.c["maskG"], ALU.mult, [psAt] + CR, [tam[i]])
                    self.mm(psO.t[:, 0:128], tam[i].t[:, :], tvb[i].t[:, :], True, False, [tam[i], tvb[i]], [psO])
                    for c2 in range(2):
                        qz = qz0[i] if c2 == 0 else qz1[i]
                        sb_ = Sb[c2]
                        self.ts("dve", sb_.t[:, :], Sst.t[:, :], tE[i].t[:, 2 * c2: 2 * c2 + 1], ALU.mult, [Sst, tE[i]], [sb_])
                        self.mm(psO.t[:, 0:128], qz.t[:, :], sb_.t[:, :], False, c2 == 1, [qz, sb_], [psO])
                        self.ts("dve", Sdec.t[:, :], Sst.t[:, :], tE[i].t[:, 4 + c2: 5 + c2], ALU.mult, [Sst, tE[i]], [Sdec])
                        prt = slice(c2 * 64, (c2 + 1) * 64)
                        self.mm(psKV.t[:, c2 * 128:(c2 + 1) * 128], tkt[i].t[prt, :], tvb[i].t[prt, :], True, True,
                                [tkt[i], tvb[i]], [psKV])
                        self.stt("dve", Sst.t[:, :], psKV.t[:, c2 * 128:(c2 + 1) * 128], tE[i].t[:, 2 * c2 + 1: 2 * c2 + 2], Sdec.t[:, :],
                                 ALU.mult, ALU.add, [psKV, tE[i], Sdec], [Sst])
                    s_ = tsm[i]
                    self.act(junk.t[:, :], psO.t[:, 0:128], AF.Square, [psO], [junk, s_], accum=s_.t[:, 0:1])
                    self.ts("dve", s_.t[:, 1:2], s_.t[:, 0:1], 1.0 / 128.0, ALU.mult, [s_], [s_], s2=NORM_EPS, op1=ALU.add)
                    self.act(s_.t[:, 1:2], s_.t[:, 1:2], AF.Ln, [s_], [s_])
                    self.act(s_.t[:, 2:3], s_.t[:, 1:2], AF.Exp, [s_], [s_], scale=-0.5)
                    self.stt("dve", ttmp[i].t[:, :], psO.t[:, 0:128], s_.t[:, 2:3], gnw.t[:, hs], ALU.mult, ALU.mult, [psO, s_, gnw], [ttmp[i]])
                    self.tt("pool", thgb[i].t[:, :], ttmp[i].t[:, :], tgg[i].t[:, :], ALU.mult, [ttmp[i], tgg[i]], [thgb[i]])
                    self.tr(psT2.t[:, 0, :], thgb[i].t[:, :], self.cb["ident"], [thgb[i]] + CR, [psT2])
                    self.act(hgT[h % 2].t[:, tb * 128:(tb + 1) * 128], psT2.t[:, 0, :], AF.Copy, [psT2], [hgT[h % 2]])

                phaseA(0)
                for tb in range(_nb):
                    if tb + 1 < _nb:
                        phaseA(tb + 1)
                    phaseB(tb)
                self.dma("sp", self.mixT_d[2048 + h * 128: 2048 + (h + 1) * 128, :], hgT[h % 2].t[:, :], [hgT[h % 2]], [])

    def stage_proj(self, l):
        pw = (self.p_m, self.p_a, self.p_g)
        with Stage(self) as st:
            hT = self.load_hT(st)
            mix = [st.sb("pmix", [128, 8, 1024], BF16) for _ in range(3)]
            wps = [st.sb("pwp", [128, 3, 8, 128], BF16) for _ in range(2)]
            wgs = [st.sb("pwg", [128, 3, KC, 128], BF16) for _ in range(2)]
            ych = [st.sb("pych", [128, 1024], BF16) for _ in range(2)]
            sgt = [st.sb("psgt", [128, 512]) for _ in range(2)]
            yacc = [st.sb("pyacc", [128, 512]) for _ in range(2)]
            tmp = [st.sb("ptmp", [128, 512]) for _ in range(2)]
            psG = [st.ps("ppsG", [128, 512]) for _ in range(2)]
            psP = [st.ps("ppsP", [128, 512]) for _ in range(2)]
            kq = 0
            cc = 0
            for th in range(2):
                for i in range(3):
                    for k8 in range(8):
                        self.dma("sp", mix[i].t[:, k8, :], self.mixT_d[i * 1024 + k8 * 128: i * 1024 + (k8 + 1) * 128, th * 1024:(th + 1) * 1024],
                                 [], [mix[i]])
                for c in range(16):
                    wp, wgt, yc = wps[cc % 2], wgs[cc % 2], ych[cc % 2]
                    cc += 1
                    for i in range(3):
                        self.dma("pool", wp.t[:, i, :, :], pw[i][l][:, c * 128:(c + 1) * 128].rearrange("(kc p) c -> p kc c", p=128), [], [wp.sub(i)])
                        o0 = O_GATE + i * D + c * 128
                        self.dma("pool", wgt.t[:, i, :, :], self.w_in[l][:, o0:o0 + 128].rearrange("(kc p) c -> p kc c", p=128), [], [wgt.sub(i)])
                    for tg in range(2):
                        ya = yacc[tg]
                        for i in range(3):
                            pg, pp, sg, tm = psG[kq % 2], psP[kq % 2], sgt[kq % 2], tmp[kq % 2]
                            kq += 1
                            t0 = th * 1024 + tg * 512
                            for kc in range(KC):
                                self.mm(pg.t[:, :], wgt.t[:, i, kc, :], hT.t[:, kc, t0:t0 + 512], kc == 0, kc == KC - 1, [wgt.sub(i), hT.sub(kc)], [pg])
                            for k8 in range(8):
                                self.mm(pp.t[:, :], wp.t[:, i, k8, :], mix[i].t[:, k8, tg * 512:(tg + 1) * 512], k8 == 0, k8 == 7, [wp.sub(i), mix[i]], [pp])
                            self.act(sg.t[:, :], pg.t[:, :], AF.Sigmoid, [pg], [sg])
                            if i == 0:
                                self.tt("dve", ya.t[:, :], pp.t[:, :], sg.t[:, :], ALU.mult, [pp, sg], [ya])
                            else:
                                self.tt("dve", tm.t[:, :], pp.t[:, :], sg.t[:, :], ALU.mult, [pp, sg], [tm])
                                if i == 1:
                                    self.tt("pool", ya.t[:, :], ya.t[:, :], tm.t[:, :], ALU.add, [ya, tm], [ya])
                                else:
                                    self.tt("pool", yc.t[:, tg * 512:(tg + 1) * 512], ya.t[:, :], tm.t[:, :], ALU.add, [ya, tm], [yc])
                    self.dma("sp", self.yT_d[c * 128:(c + 1) * 128, th * 1024:(th + 1) * 1024], yc.t[:, :], [yc], [])

    def stage_wout(self, l):
        with Stage(self) as st:
            wo = st.sb("wo", [128, KC, D], BF16)
            for kc in range(KC):
                self.dma("pool", wo.t[:, kc, :], self.w_out[l][kc * 128:(kc + 1) * 128, :], [], [wo.sub(kc)])
            lnw = self.load_bc(st, "lnw", self.ln1_w[l], D)
            lnb = self.load_bc(st, "lnb", self.ln1_b[l], D)
            a = self.ln_alloc(st)
            yT = st.sb("wyT", [128, KC, 512], BF16)
            hin = [st.sb("whin", [128, D]) for _ in range(2)]
            z = st.sb("wz", [128, D])
            psM = [st.ps("wpsM", [128, 512]) for _ in range(4)]
            yv = self.yT_d.rearrange("(kc p) t -> p kc t", p=128)
            for tb in range(NT):
                if tb % 4 == 0:
                    g = tb // 4
                    self.dma("sp", yT.t[:, :, :], yv[:, :, g * 512:(g + 1) * 512], [], [yT])
                hi = hin[tb % 2]
                if tb == 0:
                    self.dma("sp", hi.t[:, :], self.hres[0:128, :], [], [hi])
                if tb + 1 < NT:
                    hn = hin[(tb + 1) % 2]
                    self.dma("sp", hn.t[:, :], self.hres[(tb + 1) * 128:(tb + 2) * 128, :], [], [hn])
                for cg in range(4):
                    pm = psM[cg]
                    for kc in range(KC):
                        self.mm(pm.t[:, :], yT.t[:, kc, (tb % 4) * 128:(tb % 4 + 1) * 128], wo.t[:, kc, cg * 512:(cg + 1) * 512],
                                kc == 0, kc == KC - 1, [yT, wo.sub(kc)], [pm])
                    self.stt("dve", z.t[:, cg * 512:(cg + 1) * 512], hi.t[:, cg * 512:(cg + 1) * 512], ALPHA, pm.t[:, :], ALU.mult, ALU.add,
                             [hi, pm], [z])
                self.ln_block(a, z.t[:, :], [z], lnw, lnb, tb, self.hres)

    def stage_ffn(self, l):
        last = (l == DEPTH - 1)
        with Stage(self) as st:
            acc = st.sb("facc", [128, 8, D])
            for th in range(2):
                with Stage(self) as s1:
                    h1T = s1.sb("fh1T", [128, KC, 1024], BF16)
                    for kc in range(KC):
                        self.dma("sp", h1T.t[:, kc, :], self.hT_d[kc * 128:(kc + 1) * 128, th * 1024:(th + 1) * 1024], [], [h1T.sub(kc)])
                    wus = [s1.sb("fwu", [128, KC, 512], BF16) for _ in range(2)]
                    wds = [s1.sb("fwd", [128, 4, D], BF16) for _ in range(2)]
                    hids = [s1.sb("fhid", [128, 4, 1024], BF16) for _ in range(2)]
                    rts = [s1.sb("frt", [128, 512]) for _ in range(2)]
                    psU = [s1.ps("fpsU", [128, 512]) for _ in range(2)]
                    psD = [s1.ps("fpsD", [128, 512]) for _ in range(4)]
                    ku = 0
                    kd = 0
                    for sc in range(16):
                        wu, wd, hid = wus[sc % 2], wds[sc % 2], hids[sc % 2]
                        self.dma("pool", wu.t[:, :, :], self.w_up[l][:, sc * 512:(sc + 1) * 512].rearrange("(kc p) c -> p kc c", p=128), [], [wu])
                        self.dma("pool", wd.t[:, :, :], self.w_down[l][sc * 512:(sc + 1) * 512, :].rearrange("(fc p) c -> p fc c", p=128), [], [wd])
                        for fc in range(4):
                            for tg in range(2):
                                pu, rt = psU[ku % 2], rts[ku % 2]
                                ku += 1
                                for kc in range(KC):
                                    self.mm(pu.t[:, :], wu.t[:, kc, fc * 128:(fc + 1) * 128], h1T.t[:, kc, tg * 512:(tg + 1) * 512],
                                            kc == 0, kc == KC - 1, [wu, h1T.sub(kc)], [pu])
                                self.act(rt.t[:, :], pu.t[:, :], AF.Relu, [pu], [rt])
                                self.tt("dve", hid.t[:, fc, tg * 512:(tg + 1) * 512], rt.t[:, :], rt.t[:, :], ALU.mult, [rt], [hid.sub((fc, tg))])
                        hdeps = [hid.sub((fc, tg)) for fc in range(4) for tg in range(2)]
                        for tb in range(8):
                            for cg in range(4):
                                pd = psD[kd % 4]
                                kd += 1
                                for fc in range(4):
                                    self.mm(pd.t[:, :], hid.t[:, fc, tb * 128:(tb + 1) * 128], wd.t[:, fc, cg * 512:(cg + 1) * 512],
                                            fc == 0, fc == 3, hdeps + [wd], [pd])
                                dst = acc.t[:, tb, cg * 512:(cg + 1) * 512]
                                if sc == 0:
                                    self.act(dst, pd.t[:, :], AF.Copy, [pd], [acc.sub((tb, cg))])
                                else:
                                    self.tt("dve", dst, dst, pd.t[:, :], ALU.add, [pd, acc.sub((tb, cg))], [acc.sub((tb, cg))])
                with Stage(self) as s2:
                    lnw = self.load_bc(s2, "lnw", self.ln2_w[l], D)
                    lnb = self.load_bc(s2, "lnb", self.ln2_b[l], D)
                    a = self.ln_alloc(s2)
                    hin = [s2.sb("fhin", [128, D]) for _ in range(2)]
                    z = s2.sb("fz", [128, D])
                    for tb in range(8):
                        T = th * 8 + tb
                        hi = hin[tb % 2]
                        if tb == 0:
                            self.dma("sp", hi.t[:, :], self.hres[T * 128:(T + 1) * 128, :], [], [hi])
                        if tb + 1 < 8:
                            hn = hin[(tb + 1) % 2]
                            self.dma("sp", hn.t[:, :], self.hres[(T + 1) * 128:(T + 2) * 128, :], [], [hn])
                        adeps = [acc.sub((tb, cg)) for cg in range(4)]
                        self.stt("dve", z.t[:, :], hi.t[:, :], ALPHA, acc.t[:, tb, :], ALU.mult, ALU.add, [hi] + adeps, [z])
                        if last:
                            self.ln_block(a, z.t[:, :], [z], lnw, lnb, T, self.out, hT_out=False)
                        else:
                            self.ln_block(a, z.t[:, :], [z], lnw, lnb, T, self.hres)


def build_program(stages=None, dbg=False):
    nc = bass.Bass("TRN2", target_bir_lowering=False)
    k = K(nc, dbg)
    with contextlib.ExitStack() as es:
        sems = {e: es.enter_context(nc.semaphore("s_" + e)) for e in Rec.ENGS}
        dsems = [es.enter_context(nc.semaphore("d%d" % i)) for i in range(N_DMA_SEM)]
        k.setup_consts(es)
        k.R.barrier()
        if stages is None:
            stages = ["ln0"]
            for l in range(DEPTH):
                stages += ["attn%d" % l, "mlstm%d" % l, "hgrn%d" % l, "proj%d" % l, "wout%d" % l, "ffn%d" % l]
        for sname in stages:
            if sname == "ln0":
                k.stage_ln0()
            else:
                getattr(k, "stage_" + sname[:-1])(int(sname[-1]))
        k.R.barrier()
        n_ins, n_wait = k.R.emit(nc, sems, dsems)
        print("[kernel] instructions=%d waits=%d" % (n_ins, n_wait), flush=True)
    return nc, k


_PROG = None


def _get_program():
    global _PROG
    if _PROG is None:
        _PROG = build_program(None, dbg=False)
    return _PROG


def kernel(**inputs):
    nc, k = _get_program()
    _, pack, crow = host_consts()
    n_cores = 8
    shared = {}
    for n in k.declared:
        if n in ("x", "positions", "consts", "crow"):
            continue
        a = np.asarray(inputs[n])
        if n == "a_lambda":
            a = a.reshape(DEPTH, 4 * A_DH)
        shared[n] = np.ascontiguousarray(a, dtype=np.float32)
    x = np.asarray(inputs["x"], dtype=np.float32)
    pos = np.asarray(inputs["positions"]).astype(np.int32)
    in_maps = []
    for c in range(n_cores):
        b = c % x.shape[0]
        m = dict(shared)
        m["x"] = np.ascontiguousarray(x[b])
        m["positions"] = np.ascontiguousarray(pos[b:b + 1])
        m["consts"] = pack
        m["crow"] = crow
        in_maps.append({n: m[n] for n in k.declared})
    res = run_bass_kernel_spmd(nc, in_maps, core_ids=list(range(n_cores)))
    out = np.stack([np.asarray(res.results[b]["out"], dtype=np.float32) for b in range(x.shape[0])], axis=0)
    return out
```
